# Optimizing a Trainium2 kernel written in Bass

```python
import math
import jax, jax.numpy as jnp
from jax import lax
import numpy as np

D_MODEL = 1024
BATCH = 8
SEQ = 4096
DEPTH = 2
DEC_BATCH = 16
DEC_SEQ = 4096
PAST_LEN = 128

D_FF = 2816
N_EVEN = (DEPTH + 1) // 2
N_ODD = DEPTH // 2
DA = D_MODEL // 2
DB = D_MODEL - DA
CONV_A_WIDTH = 31
CONV_B_WIDTH = 3
N_HEADS = 8
HEAD_DIM = D_MODEL // N_HEADS // 2
ROPE_THETA = 10000.0
Q_BLOCK = 128
NORM_EPS = 1e-6
LN_EPS = 1e-5
SUBLN_EPS = 1e-5

kernel_name = 'hybrid_conv_diffattn_encoder'


def rmsnorm(x, g, eps=NORM_EPS):
    xf = x.astype(jnp.float32)
    y = xf * lax.rsqrt(jnp.mean(xf * xf, axis=-1, keepdims=True) + eps)
    return (y * g.astype(jnp.float32)).astype(x.dtype)


def layernorm(x, g, b, eps=LN_EPS):
    xf = x.astype(jnp.float32)
    mu = jnp.mean(xf, axis=-1, keepdims=True)
    var = jnp.mean(jnp.square(xf - mu), axis=-1, keepdims=True)
    y = (xf - mu) * lax.rsqrt(var + eps)
    return (y * g.astype(jnp.float32) + b.astype(jnp.float32)).astype(x.dtype)


def swiglu(x, w_gate, w_up, w_down):
    return (jax.nn.silu(x @ w_gate) * (x @ w_up)) @ w_down


def dwconv_centred(x, w):
    k, c = w.shape
    pad = (k - 1) // 2
    return lax.conv_general_dilated(
        x, w[:, None, :].astype(x.dtype), window_strides=(1,), padding=[(pad, pad)],
        dimension_numbers=('NWC', 'WIO', 'NWC'), feature_group_count=c)


def conv_mixers(h, w_in, a_dw_w, a_dw_b, a_ln_g, a_ln_b, b_dw_w, w_out):
    u = h @ w_in
    a_val, a_gate, g_b, g_c, hb = jnp.split(
        u, [DA, 2 * DA, 2 * DA + DB, 2 * DA + 2 * DB], axis=-1)
    a = a_val * jax.nn.sigmoid(a_gate)
    a = dwconv_centred(a, a_dw_w) + a_dw_b
    a = jax.nn.silu(layernorm(a, a_ln_g, a_ln_b))
    bm = g_b * dwconv_centred(g_c * hb, b_dw_w)
    return jnp.concatenate([a, bm], axis=-1) @ w_out


def rope(x, cos, sin):
    half = HEAD_DIM // 2
    x1, x2 = x[..., :half], x[..., half:]
    return jnp.concatenate([x1 * cos - x2 * sin, x2 * cos + x1 * sin], axis=-1)


def lambda_init_fn(layer_idx):
    return 0.8 - 0.6 * math.exp(-0.3 * layer_idx)


def diff_attention(h, w_qkv, w_out, lq1, lk1, lq2, lk2, subln_g, lambda_init):
    bsz, s, _ = h.shape
    qkv = h @ w_qkv
    q, k, v = jnp.split(qkv, 3, axis=-1)
    q = q.reshape(bsz, s, 2 * N_HEADS, HEAD_DIM)
    k = k.reshape(bsz, s, 2 * N_HEADS, HEAD_DIM)
    v = v.reshape(bsz, s, N_HEADS, 2 * HEAD_DIM)
    pos = jnp.arange(s, dtype=jnp.float32)
    inv_freq = 1.0 / (ROPE_THETA ** (jnp.arange(0, HEAD_DIM, 2, dtype=jnp.float32) / HEAD_DIM))
    ang = pos[:, None] * inv_freq[None, :]
    cos = jnp.cos(ang)[None, :, None, :].astype(h.dtype)
    sin = jnp.sin(ang)[None, :, None, :].astype(h.dtype)
    q = rope(q, cos, sin) * (HEAD_DIM ** -0.5)
    k = rope(k, cos, sin)
    lam = (jnp.exp(jnp.sum(lq1.astype(jnp.float32) * lk1.astype(jnp.float32)))
           - jnp.exp(jnp.sum(lq2.astype(jnp.float32) * lk2.astype(jnp.float32)))
           + lambda_init)
    nb = s // Q_BLOCK
    qb = q.reshape(bsz, nb, Q_BLOCK, 2 * N_HEADS, HEAD_DIM).transpose(1, 0, 2, 3, 4)

    def block(q_blk):
        sc = jnp.einsum('bqhd,bkhd->bhqk', q_blk, k).astype(jnp.float32)
        p = jax.nn.softmax(sc, axis=-1).reshape(bsz, N_HEADS, 2, Q_BLOCK, s)
        a = p[:, :, 0] - lam * p[:, :, 1]
        return jnp.einsum('bhqk,bkhe->bqhe', a.astype(v.dtype), v)

    o = lax.map(block, qb)
    o = o.transpose(1, 0, 2, 3, 4).reshape(bsz, s, N_HEADS, 2 * HEAD_DIM)
    o = rmsnorm(o, subln_g, SUBLN_EPS) * (1.0 - lambda_init)
    return o.reshape(bsz, s, N_HEADS * 2 * HEAD_DIM) @ w_out


def trunk(x, ffn1_gate, ffn1_up, ffn1_down, ffn2_gate, ffn2_up, ffn2_down,
          norm_ffn1, norm_mix, norm_ffn2,
          cv_w_in, a_dw_w, a_dw_b, a_ln_g, a_ln_b, b_dw_w, cv_w_out,
          at_w_qkv, at_w_out, lam_q1, lam_k1, lam_q2, lam_k2, subln_g, final_norm):
    for i in range(DEPTH):
        x = x + 0.5 * swiglu(rmsnorm(x, norm_ffn1[i]), ffn1_gate[i], ffn1_up[i], ffn1_down[i])
        h = rmsnorm(x, norm_mix[i])
        j = i // 2
        if i % 2 == 0:
            x = x + conv_mixers(h, cv_w_in[j], a_dw_w[j], a_dw_b[j], a_ln_g[j], a_ln_b[j],
                                b_dw_w[j], cv_w_out[j])
        else:
            x = x + diff_attention(h, at_w_qkv[j], at_w_out[j], lam_q1[j], lam_k1[j],
                                   lam_q2[j], lam_k2[j], subln_g[j], lambda_init_fn(i))
        x = x + 0.5 * swiglu(rmsnorm(x, norm_ffn2[i]), ffn2_gate[i], ffn2_up[i], ffn2_down[i])
    return rmsnorm(x, final_norm)


def setup_inputs(seed: int = 0) -> dict:
    key = jax.random.key(seed)
    ks = jax.random.split(key, 32)
    f32 = jnp.float32

    def nrm(k, shape, scale):
        return jax.random.normal(k, shape, f32) * scale

    def gain(k, shape):
        return 1.0 + 0.02 * jax.random.normal(k, shape, f32)

    D, F = D_MODEL, D_FF
    return {
        'x_prompt': jax.random.normal(ks[0], (BATCH, SEQ, D), f32),
        'x_sample': jax.random.normal(ks[1], (DEC_BATCH, DEC_SEQ, D), f32),
        'ffn1_gate': nrm(ks[2], (DEPTH, D, F), D ** -0.5),
        'ffn1_up': nrm(ks[3], (DEPTH, D, F), D ** -0.5),
        'ffn1_down': nrm(ks[4], (DEPTH, F, D), F ** -0.5),
        'ffn2_gate': nrm(ks[5], (DEPTH, D, F), D ** -0.5),
        'ffn2_up': nrm(ks[6], (DEPTH, D, F), D ** -0.5),
        'ffn2_down': nrm(ks[7], (DEPTH, F, D), F ** -0.5),
        'norm_ffn1': gain(ks[8], (DEPTH, D)),
        'norm_mix': gain(ks[9], (DEPTH, D)),
        'norm_ffn2': gain(ks[10], (DEPTH, D)),
        'cv_w_in': nrm(ks[11], (N_EVEN, D, 2 * DA + 3 * DB), D ** -0.5),
        'a_dw_w': nrm(ks[12], (N_EVEN, CONV_A_WIDTH, DA), CONV_A_WIDTH ** -0.5),
        'a_dw_b': nrm(ks[13], (N_EVEN, DA), 0.02),
        'a_ln_g': gain(ks[14], (N_EVEN, DA)),
        'a_ln_b': nrm(ks[15], (N_EVEN, DA), 0.02),
        'b_dw_w': nrm(ks[16], (N_EVEN, CONV_B_WIDTH, DB), CONV_B_WIDTH ** -0.5),
        'cv_w_out': nrm(ks[17], (N_EVEN, DA + DB, D), (DA + DB) ** -0.5),
        'at_w_qkv': nrm(ks[18], (N_ODD, D, 3 * D), D ** -0.5),
        'at_w_out': nrm(ks[19], (N_ODD, D, D), D ** -0.5),
        'lam_q1': nrm(ks[20], (N_ODD, HEAD_DIM), 0.1),
        'lam_k1': nrm(ks[21], (N_ODD, HEAD_DIM), 0.1),
        'lam_q2': nrm(ks[22], (N_ODD, HEAD_DIM), 0.1),
        'lam_k2': nrm(ks[23], (N_ODD, HEAD_DIM), 0.1),
        'subln_g': gain(ks[24], (N_ODD, 2 * HEAD_DIM)),
        'final_norm': gain(ks[25], (D,)),
    }


def reference(x_prompt, x_sample, ffn1_gate, ffn1_up, ffn1_down, ffn2_gate, ffn2_up, ffn2_down,
              norm_ffn1, norm_mix, norm_ffn2, cv_w_in, a_dw_w, a_dw_b, a_ln_g, a_ln_b, b_dw_w,
              cv_w_out, at_w_qkv, at_w_out, lam_q1, lam_k1, lam_q2, lam_k2, subln_g, final_norm):
    y_prompt = trunk(x_prompt, ffn1_gate, ffn1_up, ffn1_down, ffn2_gate, ffn2_up, ffn2_down,
                     norm_ffn1, norm_mix, norm_ffn2, cv_w_in, a_dw_w, a_dw_b, a_ln_g, a_ln_b,
                     b_dw_w, cv_w_out, at_w_qkv, at_w_out, lam_q1, lam_k1, lam_q2, lam_k2,
                     subln_g, final_norm)
    y_sample = trunk(x_sample, ffn1_gate, ffn1_up, ffn1_down, ffn2_gate, ffn2_up, ffn2_down,
                     norm_ffn1, norm_mix, norm_ffn2, cv_w_in, a_dw_w, a_dw_b, a_ln_g, a_ln_b,
                     b_dw_w, cv_w_out, at_w_qkv, at_w_out, lam_q1, lam_k1, lam_q2, lam_k2,
                     subln_g, final_norm)
    return (y_prompt, y_sample)
```

```python
import math
from contextlib import ExitStack

import numpy as np
import concourse.bass as bass
import concourse.mybir as mybir
from concourse.bass_utils import run_bass_kernel_spmd

F32 = mybir.dt.float32
BF16 = mybir.dt.bfloat16
AF = mybir.ActivationFunctionType
ALU = mybir.AluOpType
AX = mybir.AxisListType

D = 1024
DFF = 2816
NFC = 22
T = 512
NORM_EPS = 1e-6
LN_EPS = 1e-5
SUBLN_EPS = 1e-5
LAMBDA_INIT = 0.8 - 0.6 * math.exp(-0.3 * 1)
R_SLOTS = 6
SLOT_ELEMS = 2816

GROUPS = [
    ("f1gu0", 22, 2048), ("f1d0", 8, 2816), ("cin", 10, 2048),
    ("cout", 4, 2048), ("f2gu0", 22, 2048), ("f2d0", 8, 2816),
    ("f1gu1", 22, 2048), ("f1d1", 8, 2816), ("wq", 8, 2048), ("wk", 8, 2048), ("wv", 4, 2048),
    ("aout", 4, 2048), ("f2gu1", 22, 2048), ("f2d1", 8, 2816),
]
GROUP_STAGE = {"f1gu0": "A", "f1d0": "A", "cin": "A",
               "cout": "B", "f2gu0": "B", "f2d0": "B", "f1gu1": "B", "f1d1": "B",
               "wq": "B", "wk": "B", "wv": "B",
               "aout": "C", "f2gu1": "C", "f2d1": "C"}
GOFF = {}
_o = 0
for _n, _k, _e in GROUPS:
    GOFF[_n] = (_o, _k, _e)
    _o += _k * _e * 128
WTOT = _o
CAST_ITEMS = 8

CV = {}
_c = 0
for _n in ["g_f1_0", "g_mix_0", "g_f2_0", "g_f1_1", "g_mix_1", "g_f2_1", "g_fin"]:
    CV[_n] = _c; _c += 8
CV["adw"] = _c; _c += 124
CV["bdw"] = _c; _c += 12
CV["adb"] = _c; _c += 4
CV["alg"] = _c; _c += 4
CV["alb"] = _c; _c += 4
CV["subg"] = _c; _c += 1
CV["lq1"] = _c; _c += 64
CV["lk1"] = _c; _c += 64
CV["lq2"] = _c; _c += 64
CV["lk2"] = _c; _c += 64
NCV = _c


def _gu_items(gate, up):
    g = gate.reshape(8, 128, NFC, 128).transpose(2, 1, 0, 3)
    u = up.reshape(8, 128, NFC, 128).transpose(2, 1, 0, 3)
    return np.concatenate([g, u], axis=3).reshape(-1)


def _d_items(down):
    return down.reshape(NFC, 128, 8, 128).transpose(2, 1, 0, 3).reshape(-1)


def _col_items(w, n_items):
    return w.reshape(8, 128, n_items, 256).transpose(2, 1, 0, 3).reshape(-1)


def _host_pack(inp):
    parts = {}
    for l in range(2):
        parts[f"f1gu{l}"] = _gu_items(inp["ffn1_gate"][l], inp["ffn1_up"][l])
        parts[f"f1d{l}"] = _d_items(inp["ffn1_down"][l])
        parts[f"f2gu{l}"] = _gu_items(inp["ffn2_gate"][l], inp["ffn2_up"][l])
        parts[f"f2d{l}"] = _d_items(inp["ffn2_down"][l])
    w_in = inp["cv_w_in"][0]
    ch = lambda k: w_in[:, k * 128:(k + 1) * 128]
    cols = []
    for k in range(4):
        cols += [ch(k), ch(4 + k)]
    for k in range(4):
        cols += [ch(12 + k), ch(16 + k)]
    for k in range(4):
        cols += [ch(8 + k)]
    parts["cin"] = _col_items(np.concatenate(cols, axis=1), 10)
    parts["cout"] = _col_items(inp["cv_w_out"][0], 4)
    wqkv = inp["at_w_qkv"][0]
    perm = np.arange(1024).reshape(16, 64)
    perm = np.concatenate([perm[:, 32:], perm[:, :32]], axis=1).reshape(-1)
    for nm, base in (("wq", 0), ("wk", 1024)):
        w = wqkv[:, base:base + 1024]
        wp = w[:, perm]
        cols = []
        for c in range(8):
            cols += [w[:, c * 128:(c + 1) * 128], wp[:, c * 128:(c + 1) * 128]]
        parts[nm] = _col_items(np.concatenate(cols, axis=1), 8)
    parts["wv"] = _col_items(wqkv[:, 2048:3072], 4)
    parts["aout"] = _col_items(inp["at_w_out"][0], 4)
    wpack = np.concatenate([np.ascontiguousarray(parts[n]).reshape(-1) for n, _, _ in GROUPS]).astype(np.float32)
    assert wpack.size == WTOT

    cv = np.zeros((128, NCV), np.float32)
    def put8(name, vec):
        cv[:, CV[name]:CV[name] + 8] = vec.reshape(8, 128).T
    for l in range(2):
        put8(f"g_f1_{l}", inp["norm_ffn1"][l]); put8(f"g_mix_{l}", inp["norm_mix"][l]); put8(f"g_f2_{l}", inp["norm_ffn2"][l])
    put8("g_fin", inp["final_norm"])
    cv[:, CV["adw"]:CV["adw"] + 124] = inp["a_dw_w"][0].reshape(31, 4, 128).transpose(2, 0, 1).reshape(128, 124)
    cv[:, CV["bdw"]:CV["bdw"] + 12] = inp["b_dw_w"][0].reshape(3, 4, 128).transpose(2, 0, 1).reshape(128, 12)
    cv[:, CV["adb"]:CV["adb"] + 4] = inp["a_dw_b"][0].reshape(4, 128).T
    cv[:, CV["alg"]:CV["alg"] + 4] = inp["a_ln_g"][0].reshape(4, 128).T
    cv[:, CV["alb"]:CV["alb"] + 4] = inp["a_ln_b"][0].reshape(4, 128).T
    cv[:, CV["subg"]] = inp["subln_g"][0]
    for nm, key in (("lq1", "lam_q1"), ("lk1", "lam_k1"), ("lq2", "lam_q2"), ("lk2", "lam_k2")):
        cv[:, CV[nm]:CV[nm] + 64] = np.broadcast_to(inp[key][0][None, :], (128, 64))
    return wpack, cv


def _rope_tables(S):
    inv_freq = (1.0 / (10000.0 ** (np.arange(0, 64, 2, dtype=np.float32) / np.float32(64)))).astype(np.float32)
    pos = np.arange(S, dtype=np.float32)
    ang = (pos[:, None] * inv_freq[None, :]).astype(np.float32)
    cos = np.cos(ang).astype(np.float32); sin = np.sin(ang).astype(np.float32)
    p = np.arange(128)
    f = p % 32
    sign = np.where((p % 64) < 32, -1.0, 1.0).astype(np.float32)
    cosT = cos[:, f].T
    sinT = sin[:, f].T * sign[:, None]
    tab = np.stack([cosT.reshape(128, S // T, T), sinT.reshape(128, S // T, T)], axis=2)
    return np.ascontiguousarray(tab.astype(np.float32))


class _Op:
    __slots__ = ("eng", "emit", "deps", "sig", "count", "semkey", "dma_sem", "ndma")


class Sched:
    ENGS = ("pe", "act", "dve", "pool", "sp")

    def __init__(self):
        self.ops = {e: [] for e in self.ENGS}
        self.lastw = {}
        self.readers = {}

    def add(self, eng, emit, reads=(), writes=(), dma_sem=None, ndma=1):
        op = _Op()
        op.eng = eng; op.emit = emit; op.sig = False; op.count = 0
        op.dma_sem = dma_sem; op.ndma = ndma; op.semkey = dma_sem if dma_sem is not None else eng
        deps = set()
        lw = self.lastw; rd = self.readers
        for r in reads:
            w = lw.get(r)
            if w is not None:
                deps.add(w)
        for r in writes:
            w = lw.get(r)
            if w is not None:
                deps.add(w)
            for q in rd.get(r, ()):
                deps.add(q)
        deps.discard(op)
        op.deps = [d for d in deps if not (d.eng == "pe" and eng == "pe")]
        for d in op.deps:
            d.sig = True
        for r in reads:
            rd.setdefault(r, []).append(op)
        for r in writes:
            lw[r] = op
            rd[r] = []
        self.ops[eng].append(op)
        return op

    def assign(self):
        dcnt = {}
        for eng in self.ENGS:
            cnt = 0
            for op in self.ops[eng]:
                if op.dma_sem is not None:
                    c = dcnt.get(op.dma_sem, 0) + 16 * op.ndma
                    dcnt[op.dma_sem] = c
                    op.count = c
                elif op.sig:
                    cnt += 1
                    op.count = cnt

    def run(self, eng, e, sems):
        known = {}
        for op in self.ops[eng]:
            need = {}
            for d in op.deps:
                k = d.semkey
                if need.get(k, 0) < d.count:
                    need[k] = d.count
            for k, v in need.items():
                if known.get(k, 0) < v:
                    e.wait_ge(sems[k], v)
                    known[k] = v
            last = op.emit(e)
            if op.dma_sem is None and op.sig:
                last.then_inc(sems[eng], 1)


def build(NSEQ, NT, stop=None):
    S = NT * T
    NTT = NSEQ * NT
    nc = bass.Bass("TRN2", target_bir_lowering=False)
    x_d = nc.dram_tensor("x", [NSEQ * S, D], F32, kind="ExternalInput").ap()
    wpack_d = nc.dram_tensor("wpack", [WTOT], F32, kind="ExternalInput").ap()
    cvec_d = nc.dram_tensor("cvec", [128, NCV], F32, kind="ExternalInput").ap()
    ident_d = nc.dram_tensor("ident", [128, 128], F32, kind="ExternalInput").ap()
    rope_d = nc.dram_tensor("rope", [128, NT, 2, T], F32, kind="ExternalInput").ap()
    y_d = nc.dram_tensor("y", [NSEQ * S, D], F32, kind="ExternalOutput").ap()
    wsc = nc.dram_tensor("wsc", [WTOT], BF16, kind="Internal").ap()
    dsc = nc.dram_tensor("dsc", [9, 128, 2048], BF16, kind="Internal").ap()
    skind = "Internal" if stop is None else "ExternalOutput"
    xs_d = nc.dram_tensor("xs", [NTT, 128, 8 * T], F32, kind=skind).ap()
    qs_d = nc.dram_tensor("qs", [NTT, 128, 8 * T], BF16, kind=skind).ap()
    kt_d = nc.dram_tensor("kts", [NSEQ, 8, 128, S], BF16, kind=skind).ap()
    vs_d = nc.dram_tensor("vss", [NSEQ, 8, 128, S], BF16, kind=skind).ap()

    es = ExitStack()
    sb = lambda name, shape, dt: es.enter_context(nc.sbuf_tensor(name, shape, dt))
    xr = sb("xr", [128, 2, 8, T], F32)
    hT = sb("hT", [128, 8, T], BF16)
    mT = sb("mT", [128, 8, T], BF16)
    U = sb("U", [128, 24, T], BF16)
    sq = sb("sq", [128, 4, T], BF16)
    wr = sb("wr", [128, R_SLOTS, SLOT_ELEMS], BF16)
    axr = sb("axr", [128, 3, 4, 544], BF16)
    cxr = sb("cxr", [128, 3, 4, 516], BF16)
    gbr = sb("gbr", [128, 3, 4, T], BF16)
    PT = sb("PT", [128, 3, 2, T], BF16)
    TP = sb("TP", [128, 8, T], F32)
    tok = sb("tok", [128, 4, D], F32)
    ropeb = sb("ropeb", [128, 2, 2, T], F32)
    cv = sb("cv", [128, NCV], F32)
    identf = sb("identf", [128, 128], F32)
    identb = sb("identb", [128, 128], BF16)
    ones = sb("ones", [128, 4, 128], BF16)
    accD = sb("accD", [128, 2, T], F32)
    rstdb = sb("rstdb", [128, T], F32)
    onesf = sb("onesf", [128, 128], F32)
    sc = sb("sc", [128, 16], F32)
    ps = es.enter_context(nc.psum_tensor("ps", [128, 8, T], F32))

    sem_names = ["pe", "act", "dve", "pool"] + [f"w{i}" for i in range(R_SLOTS)] + \
        ["xl0", "xl1", "xst0", "xst1", "tl0", "tl1", "tl2", "tl3", "ts0", "ts1", "ts2", "ts3",
         "rope0", "rope1", "qld", "qst", "kst", "vst", "init", "dg0", "dg1", "dg2", "dg3", "dg4", "dg5", "cast0", "cast1", "cast2", "cast3"]
    sems = {n: es.enter_context(nc.semaphore(n)) for n in sem_names}

    sch = Sched()
    add = sch.add

    def bank(b):
        return ps[:, b, :]

    def col(name, k=0):
        return cv[:, CV[name] + k:CV[name] + k + 1]

    def scol(k):
        return sc[:, k:k + 1]

    tctr = [0]

    def tbuf():
        k = tctr[0] % 8
        tctr[0] += 1
        return k

    def act(out, in_, func, reads, writes, bias=None, scale=None):
        kw = {}
        if bias is not None:
            kw["bias"] = bias
        if scale is not None:
            kw["scale"] = scale
        add("act", lambda e, out=out, in_=in_, func=func, kw=kw: e.activation(out=out, in_=in_, func=func, **kw),
            reads=reads, writes=writes)

    def dve_tt(out, in0, in1, op, reads, writes):
        add("dve", lambda e, out=out, in0=in0, in1=in1, op=op: e.tensor_tensor(out=out, in0=in0, in1=in1, op=op),
            reads=reads, writes=writes)

    def dve_stt(out, in0, scalar, in1, op0, op1, reads, writes):
        add("dve", lambda e, out=out, in0=in0, scalar=scalar, in1=in1, op0=op0, op1=op1:
            e.scalar_tensor_tensor(out=out, in0=in0, scalar=scalar, in1=in1, op0=op0, op1=op1),
            reads=reads, writes=writes)

    def dve_ts(out, in0, s1, s2, op0, op1, reads, writes):
        if s2 is None:
            add("dve", lambda e, out=out, in0=in0, s1=s1, op0=op0:
                e.tensor_scalar(out=out, in0=in0, scalar1=s1, scalar2=None, op0=op0), reads=reads, writes=writes)
        else:
            add("dve", lambda e, out=out, in0=in0, s1=s1, s2=s2, op0=op0, op1=op1:
                e.tensor_scalar(out=out, in0=in0, scalar1=s1, scalar2=s2, op0=op0, op1=op1), reads=reads, writes=writes)

    def dve_recip(out, in_, reads, writes):
        add("dve", lambda e, out=out, in_=in_: e.reciprocal(out=out, in_=in_), reads=reads, writes=writes)

    def dve_copy(out, in_, reads, writes):
        add("dve", lambda e, out=out, in_=in_: e.tensor_copy(out=out, in_=in_), reads=reads, writes=writes)

    def dve_memset(ap, val, writes):
        add("dve", lambda e, ap=ap, val=val: e.memset(ap, val), writes=writes)

    def pe_mm(out, pairs, reads, writes, start=True, stop=True):
        def emit(e, out=out, pairs=pairs, start=start, stop=stop):
            n = len(pairs)
            last = None
            for k, (l, r) in enumerate(pairs):
                last = e.matmul(out, l, r, start=(start and k == 0), stop=(stop and k == n - 1))
            return last
        add("pe", emit, reads=reads, writes=writes)

    def dma(q, out, in_, sem, reads, writes):
        add(q, lambda e, out=out, in_=in_, sem=sem: e.dma_start(out=out, in_=in_).then_inc(sems[sem], 16),
            reads=reads, writes=writes, dma_sem=sem)

    wctr = [0]
    cast_done = set()

    def cast_group(g):
        off, n_items, ne = GOFF[g]
        for s0 in range(0, n_items, CAST_ITEMS):
            k = min(CAST_ITEMS, n_items - s0)
            a = off + s0 * ne * 128
            n = k * ne * 128
            cs = f"cast{len(cast_done) % 4}"
            cast_done.add((g, s0))
            dma("pool", wsc[a:a + n].rearrange("(p f) -> p f", p=128),
                wpack_d[a:a + n].rearrange("(p f) -> p f", p=128), cs,
                reads=[], writes=[("wsc", g, s0 // CAST_ITEMS), ("castsem", cs)])

    def load_item(srcs, reads):
        s = wctr[0] % R_SLOTS
        wctr[0] += 1
        def emit(e, srcs=srcs, s=s):
            last = None
            for ap, o in srcs:
                n = ap.shape[-1]
                last = e.dma_start(out=wr[:, s, o:o + n], in_=ap).then_inc(sems[f"w{s}"], 16)
            return last
        add("sp", emit, reads=reads, writes=[("w", s)], dma_sem=f"w{s}", ndma=len(srcs))
        return s

    def witem(g, idx):
        off, n_items, ne = GOFF[g]
        a = off + idx * ne * 128
        ap = wsc[a:a + ne * 128].rearrange("(p f) -> p f", p=128)
        return load_item([(ap, 0)], reads=[("wsc", g, idx // CAST_ITEMS)])

    def ditem(idx):
        n = 2048 if idx < 8 and idx % 2 == 0 else (1920 if idx < 8 else 1536)
        return load_item([(dsc[idx][:, 0:n], 0)], reads=[("dsc", idx)])

    def init_loads():
        def emit(e):
            e.dma_start(out=cv[:], in_=cvec_d[:, :]).then_inc(sems["init"], 16)
            return e.dma_start(out=identf[:], in_=ident_d[:, :]).then_inc(sems["init"], 16)
        add("pool", emit, writes=["cv", "identf"], dma_sem="init", ndma=2)

    def init():
        dve_copy(identb[:], identf[:], reads=["identf"], writes=["identb"])
        for k, v in enumerate([1.0 / 1024, 1.0 / 512, 1.0 / 128, 1.0]):
            dve_memset(ones[:, k, :], v, writes=["ones"])
        dve_memset(onesf[:], 1.0, writes=["ones"])
        dve_memset(sc[:, 8:9], NORM_EPS, writes=["sc"])
        dve_memset(sc[:, 9:10], LN_EPS, writes=["sc"])
        dve_memset(sc[:, 10:11], SUBLN_EPS, writes=["sc"])
        for j, (a, b) in enumerate((("lq1", "lk1"), ("lq2", "lk2"))):
            dve_tt(TP[:, 0, j * 64:(j + 1) * 64], cv[:, CV[a]:CV[a] + 64], cv[:, CV[b]:CV[b] + 64], ALU.mult,
                   reads=["cv"], writes=[("T", 0)])
            add("dve", lambda e, j=j: e.reduce_sum(out=sc[:, j:j + 1], in_=TP[:, 0, j * 64:(j + 1) * 64], axis=AX.X),
                reads=[("T", 0)], writes=["sc"])
        act(sc[:, 2:4], sc[:, 0:2], AF.Exp, reads=["sc"], writes=["sc"])
        dve_tt(sc[:, 4:5], sc[:, 3:4], sc[:, 2:3], ALU.subtract, reads=["sc"], writes=["sc"])
        dve_ts(sc[:, 11:12], sc[:, 4:5], -LAMBDA_INIT, None, ALU.add, None, reads=["sc"], writes=["sc"])
        dve_ts(sc[:, 12:13], col("subg"), 1.0 - LAMBDA_INIT, None, ALU.mult, None, reads=["cv", "sc"], writes=["sc"])
        Uf = U[:].rearrange("p c t -> p (c t)")
        for it in range(9):
            base = (it % 6) * 2048
            chunks = [("U", (it % 6) * 4 + q) for q in range(4)]
            if it < 8:
                ch, half = it // 2, it % 2
                taps = list(range(16)) if half == 0 else list(range(16, 31))
                cols = [CV["adw"] + k * 4 + ch for k in taps]
            else:
                cols = [CV["bdw"] + k * 4 + ch for ch in range(4) for k in range(3)]
            for m, cidx in enumerate(cols):
                dve_ts(Uf[:, base + m * 128:base + (m + 1) * 128], identb[:], cv[:, cidx:cidx + 1], None, ALU.mult, None,
                       reads=["identb", "cv"], writes=chunks)
            n = len(cols) * 128
            dma("pool", dsc[it][:, 0:n], Uf[:, base:base + n], f"dg{it % 6}", reads=chunks, writes=[("dsc", it)])

    pend = []

    def flush_pend():
        while pend:
            pend.pop(0)()

    def stat_chunk(X, c, delayed):
        k = c % 4
        act(sq[:, k, :], xr[:, X, c, :], AF.Square, reads=[("x", X, c)], writes=[("sq", k)])
        def mm(c=c, k=k):
            pe_mm(bank(6), [(ones[:, 0, :], sq[:, k, :])], reads=[("sq", k)], writes=[("ps", 6)],
                  start=(c == 0), stop=(c == 7))
        if delayed:
            pend.append(mm)
        else:
            mm()

    def stats_from_x(X):
        for c in range(8):
            stat_chunk(X, c, False)

    def norm_apply(X, gname, final=False):
        tr = tbuf()
        act(TP[:, tr, :], bank(6), AF.Ln, reads=[("ps", 6)], writes=[("T", tr)], bias=scol(8))
        act(TP[:, tr, :], TP[:, tr, :], AF.Exp, reads=[("T", tr)], writes=[("T", tr)], scale=-0.5)
        for c in range(8):
            if final:
                dve_stt(xr[:, X, c, :], xr[:, X, c, :], col(gname, c), TP[:, tr, :], ALU.mult, ALU.mult,
                        reads=[("x", X, c), ("T", tr)], writes=[("x", X, c)])
            else:
                dve_stt(hT[:, c, :], xr[:, X, c, :], col(gname, c), TP[:, tr, :], ALU.mult, ALU.mult,
                        reads=[("x", X, c), ("T", tr)], writes=[("h", c)])

    def prescale(X, i, next_g):
        add("act", lambda e, X=X, i=i, next_g=next_g: e.activation(
            out=hT[:, i, :], in_=xr[:, X, i, :], func=AF.Copy, scale=col(next_g, i)),
            reads=[("x", X, i)], writes=[("h", i)])

    def compute_rstd():
        act(rstdb[:], bank(6), AF.Ln, reads=[("ps", 6)], writes=["rstd"], bias=scol(8))
        act(rstdb[:], rstdb[:], AF.Exp, reads=["rstd"], writes=["rstd"], scale=-0.5)

    def resid_update(X, i, b, scale, next_g=None):
        dve_stt(xr[:, X, i, :], bank(b), scale, xr[:, X, i, :], ALU.mult, ALU.add,
                reads=[("ps", b), ("x", X, i)], writes=[("x", X, i)])
        if next_g is not None:
            prescale(X, i, next_g)
        stat_chunk(X, i, True)

    def ffn(X, ggu, gd, next_g=None):
        act(rstdb[:], bank(6), AF.Ln, reads=[("ps", 6)], writes=["rstd"], bias=scol(8))
        act(rstdb[:], rstdb[:], AF.Exp, reads=["rstd"], writes=["rstd"], scale=-0.5)
        for j in range(NFC):
            s = witem(ggu, j)
            bg, bu = j % 2, 2 + j % 2
            pe_mm(bank(bg), [(wr[:, s, c * 256:c * 256 + 128], hT[:, c, :]) for c in range(8)],
                  reads=[("w", s)] + [("h", c) for c in range(8)], writes=[("ps", bg)])
            pe_mm(bank(bu), [(wr[:, s, c * 256 + 128:c * 256 + 256], hT[:, c, :]) for c in range(8)],
                  reads=[("w", s)] + [("h", c) for c in range(8)], writes=[("ps", bu)])
            tg = tbuf(); tu = tbuf()
            dve_tt(TP[:, tg, :], bank(bg), rstdb[:], ALU.mult, reads=[("ps", bg), "rstd"], writes=[("T", tg)])
            act(TP[:, tg, :], TP[:, tg, :], AF.Silu, reads=[("T", tg)], writes=[("T", tg)])
            dve_tt(TP[:, tu, :], bank(bu), rstdb[:], ALU.mult, reads=[("ps", bu), "rstd"], writes=[("T", tu)])
            dve_tt(U[:, j, :], TP[:, tu, :], TP[:, tg, :], ALU.mult, reads=[("T", tu), ("T", tg)], writes=[("U", j)])
        for i in range(8):
            s = witem(gd, i)
            b = 4 + i % 2
            pe_mm(bank(b), [(wr[:, s, j * 128:(j + 1) * 128], U[:, j, :]) for j in range(NFC)],
                  reads=[("w", s)] + [("U", j) for j in range(NFC)], writes=[("ps", b)])
            if len(pend) > 0:
                pend.pop(0)()
            resid_update(X, i, b, 0.5, next_g)
        flush_pend()

    def proj_resid(X, g, src, srckey, next_g=None):
        for n in range(4):
            s = witem(g, n)
            for hh in range(2):
                i = 2 * n + hh
                b = 4 + i % 2
                pe_mm(bank(b), [(wr[:, s, kc * 256 + hh * 128:kc * 256 + hh * 128 + 128], src[:, kc, :]) for kc in range(8)],
                      reads=[("w", s)] + [(srckey, kc) for kc in range(8)], writes=[("ps", b)])
                if len(pend) > 0:
                    pend.pop(0)()
                resid_update(X, i, b, 1.0, next_g)
        flush_pend()

    def load_tok(seq, i):
        for blk in range(4):
            r0 = seq * S + i * T + blk * 128
            dma("pool", tok[:, blk, :], x_d[r0:r0 + 128, :], f"tl{blk}", reads=[], writes=[("tok", blk)])

    def load_x(seq, i, X):
        dma("pool", xr[:, X].rearrange("p c t -> p (c t)"), xs_d[seq * NT + i], f"xl{X}",
            reads=[("xs", seq, i)], writes=[("x", X, c) for c in range(8)])

    def store_x(seq, i, X):
        dma("pool", xs_d[seq * NT + i], xr[:, X].rearrange("p c t -> p (c t)"), f"xst{X}",
            reads=[("x", X, c) for c in range(8)], writes=[("xs", seq, i)])

    def load_rope(i, rb):
        dma("pool", ropeb[:, rb], rope_d[:, i], f"rope{rb}", reads=[], writes=[("rope", rb)])

    def load_q(seq, i):
        dma("pool", hT[:].rearrange("p c t -> p (c t)"), qs_d[seq * NT + i], "qld",
            reads=[("qs", seq, i)], writes=[("h", c) for c in range(8)])

    def stage_A(seq, i, X, prefetch):
        r = i % 3
        for blk in range(4):
            for half in range(2):
                b = 4 + half
                def emit(e, blk=blk, half=half, b=b):
                    last = None
                    for cc in range(4):
                        c = half * 4 + cc
                        last = e.transpose(ps[:, b, cc * 128:(cc + 1) * 128], tok[:, blk, c * 128:(c + 1) * 128], identf[:])
                    return last
                add("pe", emit, reads=[("tok", blk)], writes=[("ps", b)])
                dst = xr[:, X, half * 4:half * 4 + 4, blk * 128:(blk + 1) * 128]
                src = ps[:, b, :].rearrange("p (c t) -> p c t", c=4)
                wl = [("x", X, half * 4 + cc) for cc in range(4)]
                if half == 0:
                    add("act", lambda e, dst=dst, src=src: e.activation(out=dst, in_=src, func=AF.Copy),
                        reads=[("ps", b)], writes=wl)
                else:
                    dve_copy(dst, src, reads=[("ps", b)], writes=wl)
        prefetch("after_tok")
        for c in range(8):
            if c % 2 == 0:
                prescale(X, c, "g_f1_0")
            else:
                dve_ts(hT[:, c, :], xr[:, X, c, :], col("g_f1_0", c), None, ALU.mult, None,
                       reads=[("x", X, c)], writes=[("h", c)])
        stats_from_x(X)
        ffn(X, "f1gu0", "f1d0", next_g="g_mix_0")
        compute_rstd()
        hreads = [("h", c) for c in range(8)]
        for k in range(4):
            s = witem("cin", k)
            bv, bgt = k % 2, 2 + k % 2
            pe_mm(bank(bv), [(wr[:, s, c * 256:c * 256 + 128], hT[:, c, :]) for c in range(8)],
                  reads=[("w", s)] + hreads, writes=[("ps", bv)])
            pe_mm(bank(bgt), [(wr[:, s, c * 256 + 128:c * 256 + 256], hT[:, c, :]) for c in range(8)],
                  reads=[("w", s)] + hreads, writes=[("ps", bgt)])
            tg = tbuf(); tv = tbuf()
            dve_tt(TP[:, tg, :], bank(bgt), rstdb[:], ALU.mult, reads=[("ps", bgt), "rstd"], writes=[("T", tg)])
            act(TP[:, tg, :], TP[:, tg, :], AF.Sigmoid, reads=[("T", tg)], writes=[("T", tg)])
            dve_tt(TP[:, tv, :], bank(bv), rstdb[:], ALU.mult, reads=[("ps", bv), "rstd"], writes=[("T", tv)])
            dve_tt(axr[:, r, k, 15:15 + T], TP[:, tv, :], TP[:, tg, :], ALU.mult,
                   reads=[("T", tv), ("T", tg)], writes=[("ax", r)])
        for k in range(4):
            s = witem("cin", 4 + k)
            bc, bh = k % 2, 2 + k % 2
            pe_mm(bank(bc), [(wr[:, s, c * 256:c * 256 + 128], hT[:, c, :]) for c in range(8)],
                  reads=[("w", s)] + hreads, writes=[("ps", bc)])
            pe_mm(bank(bh), [(wr[:, s, c * 256 + 128:c * 256 + 256], hT[:, c, :]) for c in range(8)],
                  reads=[("w", s)] + hreads, writes=[("ps", bh)])
            t = tbuf()
            dve_tt(TP[:, t, :], bank(bc), rstdb[:], ALU.mult, reads=[("ps", bc), "rstd"], writes=[("T", t)])
            dve_tt(TP[:, t, :], TP[:, t, :], rstdb[:], ALU.mult, reads=[("T", t), "rstd"], writes=[("T", t)])
            dve_tt(cxr[:, r, k, 1:1 + T], TP[:, t, :], bank(bh), ALU.mult,
                   reads=[("T", t), ("ps", bh)], writes=[("cx", r)])
        for n in range(2):
            s = witem("cin", 8 + n)
            for hh in range(2):
                b = 2 * (n % 2) + hh
                pe_mm(bank(b), [(wr[:, s, c * 256 + hh * 128:c * 256 + hh * 128 + 128], hT[:, c, :]) for c in range(8)],
                      reads=[("w", s)] + hreads, writes=[("ps", b)])
                dve_tt(gbr[:, r, 2 * n + hh, :], bank(b), rstdb[:], ALU.mult, reads=[("ps", b), "rstd"], writes=[("gb", r)])
        if i == 0:
            dve_memset(axr[:, r, :, 0:15], 0.0, writes=[("ax", r)])
            dve_memset(cxr[:, r, :, 0:1], 0.0, writes=[("cx", r)])
        else:
            rp = (i - 1) % 3
            dve_copy(axr[:, rp, :, 15 + T:30 + T], axr[:, r, :, 15:30], reads=[("ax", r)], writes=[("ax", rp)])
            dve_copy(cxr[:, rp, :, 1 + T:2 + T], cxr[:, r, :, 1:2], reads=[("cx", r)], writes=[("cx", rp)])
        if i == NT - 1:
            dve_memset(axr[:, r, :, 15 + T:30 + T], 0.0, writes=[("ax", r)])
            dve_memset(cxr[:, r, :, 1 + T:2 + T], 0.0, writes=[("cx", r)])
        else:
            rn = (i + 1) % 3
            dve_copy(axr[:, rn, :, 0:15], axr[:, r, :, T:T + 15], reads=[("ax", r)], writes=[("ax", rn)])
            dve_copy(cxr[:, rn, :, 0:1], cxr[:, r, :, T:T + 1], reads=[("cx", r)], writes=[("cx", rn)])
        store_x(seq, i, X)

    def stage_B(seq, i, X, rb, prefetch):
        r = i % 3
        prefetch("start")
        for ch in range(4):
            s0 = ditem(2 * ch)
            s1 = ditem(2 * ch + 1)
            pairs = [(wr[:, s0, k * 128:(k + 1) * 128], axr[:, r, ch, k:k + T]) for k in range(16)]
            pairs += [(wr[:, s1, (k - 16) * 128:(k - 15) * 128], axr[:, r, ch, k:k + T]) for k in range(16, 31)]
            pe_mm(bank(ch), pairs, reads=[("w", s0), ("w", s1), ("ax", r)], writes=[("ps", ch)])
        for ch in range(4):
            k0, k1 = ch % 2, 2 + ch % 2
            act(sq[:, k0, :], bank(ch), AF.Identity, reads=[("ps", ch)], writes=[("sq", k0)], bias=col("adb", ch))
            pe_mm(bank(6), [(ones[:, 1, :], sq[:, k0, :])], reads=[("sq", k0)], writes=[("ps", 6)],
                  start=(ch == 0), stop=(ch == 3))
            act(sq[:, k1, :], bank(ch), AF.Square, reads=[("ps", ch)], writes=[("sq", k1)], bias=col("adb", ch))
            pe_mm(bank(7), [(ones[:, 1, :], sq[:, k1, :])], reads=[("sq", k1)], writes=[("ps", 7)],
                  start=(ch == 0), stop=(ch == 3))
        tm = tbuf(); tv = tbuf(); trs = tbuf()
        act(TP[:, tm, :], bank(6), AF.Copy, reads=[("ps", 6)], writes=[("T", tm)])
        dve_tt(TP[:, tv, :], TP[:, tm, :], TP[:, tm, :], ALU.mult, reads=[("T", tm)], writes=[("T", tv)])
        dve_tt(TP[:, tv, :], bank(7), TP[:, tv, :], ALU.subtract, reads=[("ps", 7), ("T", tv)], writes=[("T", tv)])
        dve_ts(TP[:, tv, :], TP[:, tv, :], 0.0, LN_EPS, ALU.max, ALU.add, reads=[("T", tv)], writes=[("T", tv)])
        act(TP[:, tv, :], TP[:, tv, :], AF.Ln, reads=[("T", tv)], writes=[("T", tv)])
        act(TP[:, trs, :], TP[:, tv, :], AF.Exp, reads=[("T", tv)], writes=[("T", trs)], scale=-0.5)
        for ch in range(4):
            td = tbuf()
            dve_stt(TP[:, td, :], bank(ch), col("adb", ch), TP[:, tm, :], ALU.add, ALU.subtract,
                    reads=[("ps", ch), ("T", tm)], writes=[("T", td)])
            dve_tt(TP[:, td, :], TP[:, td, :], TP[:, trs, :], ALU.mult, reads=[("T", td), ("T", trs)], writes=[("T", td)])
            act(mT[:, ch, :], TP[:, td, :], AF.Silu, reads=[("T", td)], writes=[("m", ch)],
                bias=col("alb", ch), scale=col("alg", ch))
        sB = ditem(8)
        for ch in range(4):
            b = 4 + ch % 2
            pe_mm(bank(b), [(wr[:, sB, (ch * 3 + k) * 128:(ch * 3 + k + 1) * 128], cxr[:, r, ch, k:k + T]) for k in range(3)],
                  reads=[("w", sB), ("cx", r)], writes=[("ps", b)])
            dve_tt(mT[:, 4 + ch, :], bank(b), gbr[:, r, ch, :], ALU.mult,
                   reads=[("ps", b), ("gb", r)], writes=[("m", 4 + ch)])
        proj_resid(X, "cout", mT, "m", next_g="g_f2_0")
        ffn(X, "f2gu0", "f2d0", next_g="g_f1_1")
        ffn(X, "f1gu1", "f1d1", next_g="g_mix_1")
        compute_rstd()
        for j_ in range(2):
            dve_tt(ropeb[:, rb, j_, :], ropeb[:, rb, j_, :], rstdb[:], ALU.mult,
                   reads=[("rope", rb), "rstd"], writes=[("rope", rb)])
        hreads = [("h", c) for c in range(8)]
        for which, g in enumerate(("wq", "wk")):
            for c in range(8):
                s = witem(g, c)
                b0, b1 = c % 2, 2 + c % 2
                pe_mm(bank(b0), [(wr[:, s, kc * 256:kc * 256 + 128], hT[:, kc, :]) for kc in range(8)],
                      reads=[("w", s)] + hreads, writes=[("ps", b0)])
                pe_mm(bank(b1), [(wr[:, s, kc * 256 + 128:kc * 256 + 256], hT[:, kc, :]) for kc in range(8)],
                      reads=[("w", s)] + hreads, writes=[("ps", b1)])
                t1 = tbuf(); t2 = tbuf()
                dve_tt(TP[:, t1, :], bank(b0), ropeb[:, rb, 0, :], ALU.mult, reads=[("ps", b0), ("rope", rb)], writes=[("T", t1)])
                dve_tt(TP[:, t2, :], bank(b1), ropeb[:, rb, 1, :], ALU.mult, reads=[("ps", b1), ("rope", rb)], writes=[("T", t2)])
                dve_tt(U[:, which * 8 + c, :], TP[:, t1, :], TP[:, t2, :], ALU.add,
                       reads=[("T", t1), ("T", t2)], writes=[("U", which * 8 + c)])
                if which == 0:
                    dve_tt(mT[:, c, :], hT[:, c, :], rstdb[:], ALU.mult, reads=[("h", c), "rstd"], writes=[("m", c)])
        for n in range(4):
            s = witem("wv", n)
            for bp in range(2):
                b = 4 + (2 * n + bp) % 2
                def emit(e, s=s, bp=bp, b=b):
                    last = None
                    for bb in range(2):
                        blk = 2 * bp + bb
                        for kc in range(8):
                            last = e.matmul(ps[:, b, bb * 256:(bb + 1) * 256], mT[:, kc, blk * 128:(blk + 1) * 128],
                                            wr[:, s, kc * 256:(kc + 1) * 256], start=(kc == 0), stop=(kc == 7))
                    return last
                add("pe", emit, reads=[("w", s)] + [("m", c_) for c_ in range(8)], writes=[("ps", b)])
                for bb in range(2):
                    blk = 2 * bp + bb
                    src = ps[:, b, bb * 256:(bb + 1) * 256].rearrange("p (h e) -> p h e", h=2)
                    dst = U[:, 16 + 2 * n:16 + 2 * n + 2, blk * 128:(blk + 1) * 128]
                    wl = [("U", 16 + 2 * n), ("U", 16 + 2 * n + 1)]
                    if bb == 0:
                        add("act", lambda e, dst=dst, src=src: e.activation(out=dst, in_=src, func=AF.Copy),
                            reads=[("ps", b)], writes=wl)
                    else:
                        dve_copy(dst, src, reads=[("ps", b)], writes=wl)
        dma("pool", qs_d[seq * NT + i], U[:, 0:8, :].rearrange("p c t -> p (c t)"), "qst",
            reads=[("U", c) for c in range(8)], writes=[("qs", seq, i)])
        dma("pool", kt_d[seq][:, :, i * T:(i + 1) * T].rearrange("h p t -> p h t"), U[:, 8:16, :], "kst",
            reads=[("U", c) for c in range(8, 16)], writes=[("K", seq, i)])
        dma("pool", vs_d[seq][:, :, i * T:(i + 1) * T].rearrange("h p t -> p h t"), U[:, 16:24, :], "vst",
            reads=[("U", c) for c in range(16, 24)], writes=[("V", seq, i)])
        store_x(seq, i, X)
        prefetch("end")

    def stage_C(seq, i, X, prefetch):
        prefetch("start")
        NKB = S // 128
        PCS = 8
        pctr = [0]
        deferred = []
        for h in range(8):
            qreads = [("U", h)]
            slots = {}
            def get_piece(pc, h=h, slots=slots):
                if pc not in slots:
                    k0 = pc * PCS * 128
                    n = min(PCS * 128, S - k0)
                    tiles = sorted(set([(k0 + j) // T for j in range(0, n, 128)]))
                    slots[pc] = load_item([(kt_d[seq][h][:, k0:k0 + n], 0), (vs_d[seq][h][:, k0:k0 + n], 1024)],
                                          reads=[("K", seq, t) for t in tiles] + [("V", seq, t) for t in tiles])
                return slots[pc]

            def emit_S(kb, h=h):
                s = get_piece(kb // PCS)
                kk = (kb % PCS) * 128
                sbuf_ = kb % 2
                def emit(e, s=s, kk=kk, sbuf_=sbuf_, h=h):
                    e.matmul(ps[:, 2 * sbuf_, :], wr[0:64, s, kk:kk + 128], hT[0:64, h, :], start=True, stop=True)
                    return e.matmul(ps[:, 2 * sbuf_ + 1, :], wr[64:128, s, kk:kk + 128], hT[64:128, h, :], start=True, stop=True)
                add("pe", emit, reads=[("w", s), ("h", h)], writes=[("ps", 2 * sbuf_), ("ps", 2 * sbuf_ + 1)])

            def emit_PV(kb, h=h):
                s = get_piece(kb // PCS)
                kk = 1024 + (kb % PCS) * 128
                sbuf_ = kb % 2
                pb = pctr[0] % 3
                pctr[0] += 1
                add("act", lambda e, pb=pb, sbuf_=sbuf_: e.activation(
                    out=PT[:, pb], in_=ps[:, 2 * sbuf_:2 * sbuf_ + 2, :], func=AF.Exp, scale=0.125),
                    reads=[("ps", 2 * sbuf_), ("ps", 2 * sbuf_ + 1)], writes=[("P", pb)])
                if kb + 2 < NKB:
                    emit_S(kb + 2)
                def emit(e, s=s, kk=kk, pb=pb, kb=kb):
                    st = (kb == 0); sp_ = (kb == NKB - 1)
                    e.matmul(ps[:, 4, :], wr[:, s, kk:kk + 128], PT[:, pb, 0, :], start=st, stop=sp_)
                    return e.matmul(ps[:, 5, :], wr[:, s, kk:kk + 128], PT[:, pb, 1, :], start=st, stop=sp_)
                add("pe", emit, reads=[("w", s), ("P", pb)], writes=[("ps", 4), ("ps", 5)])
                if kb % 4 == 3:
                    def emit2(e, pb=pb, kb=kb):
                        st = (kb == 3)
                        e.matmul(ps[:, 6, :], ones[:, 3, :], PT[:, pb, 0, :], start=st, stop=False)
                        return e.matmul(ps[:, 7, :], ones[:, 3, :], PT[:, pb, 1, :], start=st, stop=False)
                    add("pe", emit2, reads=[("P", pb)], writes=[("ps", 6), ("ps", 7)])
                elif kb == 0:
                    add("dve", lambda e, pb=pb: e.tensor_copy(out=accD[:], in_=PT[:, pb]),
                        reads=[("P", pb)], writes=["accD"])
                else:
                    add("dve", lambda e, pb=pb: e.tensor_tensor(out=accD[:], in0=accD[:], in1=PT[:, pb], op=ALU.add),
                        reads=[("P", pb), "accD"], writes=["accD"])

            emit_S(0)
            if NKB > 1:
                emit_S(1)
            for kb in range(NKB):
                emit_PV(kb)
                if kb == 1 and deferred:
                    deferred.pop(0)()
            o0 = tbuf(); o1 = tbuf()
            dve_copy(TP[:, o0, :], bank(4), reads=[("ps", 4)], writes=[("T", o0)])
            act(TP[:, o1, :], bank(5), AF.Copy, reads=[("ps", 5)], writes=[("T", o1)])
            k1 = 2 * (h % 2)
            add("act", lambda e, k1=k1: e.activation(out=sq[:, k1:k1 + 2, :], in_=accD[:], func=AF.Copy),
                reads=["accD"], writes=[("sq", k1), ("sq", k1 + 1)])
            for m_ in range(2):
                def emit3(e, m_=m_, k1=k1):
                    return e.matmul(ps[:, 6 + m_, :], ones[:, 3, :], sq[:, k1 + m_, :], start=(NKB < 4), stop=True)
                add("pe", emit3, reads=[("sq", k1 + m_)], writes=[("ps", 6 + m_)])

            def part2(h=h, o0=o0, o1=o1):
                r0 = tbuf(); r1 = tbuf(); to = tbuf(); tl = tbuf()
                act(TP[:, r0, :], bank(6), AF.Ln, reads=[("ps", 6)], writes=[("T", r0)])
                act(TP[:, r0, :], TP[:, r0, :], AF.Exp, reads=[("T", r0)], writes=[("T", r0)], scale=-1.0)
                act(TP[:, r1, :], bank(7), AF.Ln, reads=[("ps", 7)], writes=[("T", r1)])
                act(TP[:, r1, :], TP[:, r1, :], AF.Exp, reads=[("T", r1)], writes=[("T", r1)], scale=-1.0)
                dve_tt(TP[:, o0, :], TP[:, o0, :], TP[:, r0, :], ALU.mult, reads=[("T", o0), ("T", r0)], writes=[("T", o0)])
                dve_tt(TP[:, o1, :], TP[:, o1, :], TP[:, r1, :], ALU.mult, reads=[("T", o1), ("T", r1)], writes=[("T", o1)])
                dve_stt(TP[:, to, :], TP[:, o1, :], scol(11), TP[:, o0, :], ALU.mult, ALU.add,
                        reads=[("T", o1), ("T", o0)], writes=[("T", to)])
                k = (2 * (h % 2) + 2) % 4
                dve_tt(sq[:, k, :], TP[:, to, :], TP[:, to, :], ALU.mult, reads=[("T", to)], writes=[("sq", k)])
                pe_mm(bank(6), [(ones[:, 2, :], sq[:, k, :])], reads=[("sq", k)], writes=[("ps", 6)])
                act(TP[:, tl, :], bank(6), AF.Ln, reads=[("ps", 6)], writes=[("T", tl)], bias=scol(10))
                act(TP[:, tl, :], TP[:, tl, :], AF.Exp, reads=[("T", tl)], writes=[("T", tl)], scale=-0.5)
                dve_stt(mT[:, h, :], TP[:, to, :], scol(12), TP[:, tl, :], ALU.mult, ALU.mult,
                        reads=[("T", to), ("T", tl)], writes=[("m", h)])
            deferred.append(part2)
        while deferred:
            deferred.pop(0)()
        proj_resid(X, "aout", mT, "m", next_g="g_f2_1")
        if stop == "C":
            store_x(seq, i, X)
            dma("pool", qs_d[seq * NT + i], mT[:].rearrange("p c t -> p (c t)"), "qst",
                reads=[("m", c) for c in range(8)], writes=[("qs", seq, i)])
            prefetch("after_ffn")
            prefetch("end")
            return
        ffn(X, "f2gu1", "f2d1")
        prefetch("after_ffn")
        norm_apply(X, "g_fin", final=True)
        for blk in range(4):
            for half in range(2):
                b = 4 + half
                def emit(e, blk=blk, half=half, b=b):
                    last = None
                    for cc in range(4):
                        c = half * 4 + cc
                        last = e.transpose(ps[:, b, cc * 128:(cc + 1) * 128], xr[:, X, c, blk * 128:(blk + 1) * 128], identf[:])
                    return last
                add("pe", emit, reads=[("x", X, half * 4 + cc) for cc in range(4)], writes=[("ps", b)])
                dst = tok[:, blk, half * 512:(half + 1) * 512]
                if half == 0:
                    add("act", lambda e, dst=dst, b=b: e.activation(out=dst, in_=ps[:, b, :], func=AF.Copy),
                        reads=[("ps", b)], writes=[("tokh", blk, 0)])
                else:
                    dve_copy(dst, ps[:, b, :], reads=[("ps", b)], writes=[("tokh", blk, 1)])
            r0 = seq * S + i * T + blk * 128
            dma("pool", y_d[r0:r0 + 128, :], tok[:, blk, :], f"ts{blk}",
                reads=[("tok", blk), ("tokh", blk, 0), ("tokh", blk, 1)], writes=[("y", seq, i, blk), ("tok", blk)])
        prefetch("end")

    calls = []
    for seq in range(NSEQ):
        order = []
        for i in range(NT):
            order.append(("A", i))
            if i >= 1:
                order.append(("B", i - 1))
        order.append(("B", NT - 1))
        for i in range(NT):
            order.append(("C", i))
        calls += [(k, seq, i) for k, i in order]
    if stop == "A":
        calls = [c for c in calls if c[0] == "A"]
    elif stop == "B":
        calls = [c for c in calls if c[0] in ("A", "B")]

    init_loads()
    load_tok(calls[0][1], calls[0][2])
    for g, _, _ in GROUPS:
        if GROUP_STAGE[g] == "A":
            cast_group(g)
    init()
    add("pe", lambda e: e.nop(), reads=["identf", "identb", "ones", "cv", "sc"])
    add("act", lambda e: e.nop(), reads=["identf", "identb", "ones", "cv", "sc"])
    nB = [0]
    bcast = [False]
    ccast = [False]

    for n, (kind, seq, i) in enumerate(calls):
        X = n % 2
        nxt = calls[n + 1] if n + 1 < len(calls) else None
        rb_cur = nB[0] % 2

        def prefetch(point, kind=kind, nxt=nxt, n=n):
            if nxt is None:
                return
            nk, ns, ni = nxt
            NX = (n + 1) % 2
            if nk == "A":
                want = {"A": "after_tok", "B": "start", "C": "end"}[kind]
                if point == want:
                    load_tok(ns, ni)
            elif nk == "B":
                if point == "start" or (kind == "A" and point == "after_tok"):
                    load_x(ns, ni, NX)
                    load_rope(ni, (nB[0] + (1 if kind == "B" else 0)) % 2)
            elif nk == "C":
                if point == "start":
                    load_x(ns, ni, NX)
                if (kind == "B" and point == "end") or (kind == "C" and point == "after_ffn"):
                    load_q(ns, ni)
            if kind == "A" and point == "after_tok" and not bcast[0]:
                bcast[0] = True
                for g, _, _ in GROUPS:
                    if GROUP_STAGE[g] == "B":
                        cast_group(g)
            elif kind == "A" and point == "after_tok" and bcast[0] and not ccast[0] and n >= 2:
                ccast[0] = True
                for g, _, _ in GROUPS:
                    if GROUP_STAGE[g] == "C":
                        cast_group(g)

        if kind == "A":
            stage_A(seq, i, X, prefetch)
        elif kind == "B":
            stage_B(seq, i, X, rb_cur, prefetch)
            nB[0] += 1
        else:
            if not ccast[0]:
                ccast[0] = True
                for g, _, _ in GROUPS:
                    if GROUP_STAGE[g] == "C":
                        cast_group(g)
            stage_C(seq, i, X, prefetch)

    fin = [("tok", b) for b in range(4)]
    if stop is not None:
        fin += [("xs", q, w) for q in range(NSEQ) for w in range(NT)] + [("qs", q, w) for q in range(NSEQ) for w in range(NT)]
        fin += [("K", q, w) for q in range(NSEQ) for w in range(NT)] + [("V", q, w) for q in range(NSEQ) for w in range(NT)]
    add("pool", lambda e: None, reads=fin, writes=fin)

    sch.assign()
    with nc.Block() as block:
        @block.tensor
        def _(e):
            sch.run("pe", e, sems)

        @block.scalar
        def _(e):
            sch.run("act", e, sems)

        @block.vector
        def _(e):
            sch.run("dve", e, sems)

        @block.gpsimd
        def _(e):
            sch.run("pool", e, sems)

        @block.sync
        def _(e):
            sch.run("sp", e, sems)
    es.close()
    return nc


_IDENT = np.eye(128, dtype=np.float32)


def run_cores(xs_per_core, inp, NSEQ, NT, stop=None):
    wpack, cvec = _host_pack(inp)
    rope = _rope_tables(NT * T)
    nc = build(NSEQ, NT, stop)
    in_maps = [{"x": np.ascontiguousarray(xc.reshape(NSEQ * NT * T, D)), "wpack": wpack, "cvec": cvec,
                "ident": _IDENT, "rope": rope} for xc in xs_per_core]
    res = run_bass_kernel_spmd(nc, in_maps, core_ids=list(range(len(xs_per_core))))
    if stop is not None:
        return res.results
    return [r["y"].reshape(NSEQ, NT * T, D) for r in res.results]


def kernel(**inputs):
    inp = {k: np.asarray(v) for k, v in inputs.items()}
    xp, xsm = inp["x_prompt"], inp["x_sample"]
    per_core = [np.stack([xp[c], xsm[2 * c], xsm[2 * c + 1]], axis=0) for c in range(8)]
    outs = run_cores(per_core, inp, 3, 8)
    y_prompt = np.stack([outs[c][0] for c in range(8)], axis=0).astype(np.float32)
    y_sample = np.stack([outs[c // 2][1 + c % 2] for c in range(16)], axis=0).astype(np.float32)
    return (y_prompt, y_sample)
```

```python
import math
from contextlib import ExitStack

import numpy as np
import concourse.bass as bass
import concourse.mybir as mybir
from concourse.bass_utils import run_bass_kernel_spmd

F32 = mybir.dt.float32
BF16 = mybir.dt.bfloat16
AF = mybir.ActivationFunctionType
ALU = mybir.AluOpType
AX = mybir.AxisListType

D = 1024
DFF = 2816
NFC = 22
T = 512
NORM_EPS = 1e-6
LN_EPS = 1e-5
SUBLN_EPS = 1e-5
LAMBDA_INIT = 0.8 - 0.6 * math.exp(-0.3 * 1)
R_SLOTS = 6
SLOT_ELEMS = 2816

GROUPS = [
    ("f1gu0", 22, 2048), ("f1d0", 8, 2816), ("cin", 10, 2048),
    ("cout", 4, 2048), ("f2gu0", 22, 2048), ("f2d0", 8, 2816),
    ("f1gu1", 22, 2048), ("f1d1", 8, 2816), ("wq", 8, 2048), ("wk", 8, 2048), ("wv", 4, 2048),
    ("aout", 4, 2048), ("f2gu1", 22, 2048), ("f2d1", 8, 2816),
]
GROUP_STAGE = {"f1gu0": "A", "f1d0": "A", "cin": "A",
               "cout": "B", "f2gu0": "B", "f2d0": "B", "f1gu1": "B", "f1d1": "B",
               "wq": "B", "wk": "B", "wv": "B",
               "aout": "C", "f2gu1": "C", "f2d1": "C"}
GOFF = {}
_o = 0
for _n, _k, _e in GROUPS:
    GOFF[_n] = (_o, _k, _e)
    _o += _k * _e * 128
WTOT = _o
CAST_ITEMS = 8

CV = {}
_c = 0
for _n in ["g_f1_0", "g_mix_0", "g_f2_0", "g_f1_1", "g_mix_1", "g_f2_1", "g_fin"]:
    CV[_n] = _c; _c += 8
CV["adw"] = _c; _c += 124
CV["bdw"] = _c; _c += 12
CV["adb"] = _c; _c += 4
CV["alg"] = _c; _c += 4
CV["alb"] = _c; _c += 4
CV["subg"] = _c; _c += 1
CV["lq1"] = _c; _c += 64
CV["lk1"] = _c; _c += 64
CV["lq2"] = _c; _c += 64
CV["lk2"] = _c; _c += 64
NCV = _c


def _gu_items(gate, up):
    g = gate.reshape(8, 128, NFC, 128).transpose(2, 1, 0, 3)
    u = up.reshape(8, 128, NFC, 128).transpose(2, 1, 0, 3)
    return np.concatenate([g, u], axis=3).reshape(-1)


def _d_items(down):
    return down.reshape(NFC, 128, 8, 128).transpose(2, 1, 0, 3).reshape(-1)


def _col_items(w, n_items):
    return w.reshape(8, 128, n_items, 256).transpose(2, 1, 0, 3).reshape(-1)


def _host_pack(inp):
    parts = {}
    for l in range(2):
        parts[f"f1gu{l}"] = _gu_items(inp["ffn1_gate"][l], inp["ffn1_up"][l])
        parts[f"f1d{l}"] = _d_items(inp["ffn1_down"][l])
        parts[f"f2gu{l}"] = _gu_items(inp["ffn2_gate"][l], inp["ffn2_up"][l])
        parts[f"f2d{l}"] = _d_items(inp["ffn2_down"][l])
    w_in = inp["cv_w_in"][0]
    ch = lambda k: w_in[:, k * 128:(k + 1) * 128]
    cols = []
    for k in range(4):
        cols += [ch(k), ch(4 + k)]
    for k in range(4):
        cols += [ch(12 + k), ch(16 + k)]
    for k in range(4):
        cols += [ch(8 + k)]
    parts["cin"] = _col_items(np.concatenate(cols, axis=1), 10)
    parts["cout"] = _col_items(inp["cv_w_out"][0], 4)
    wqkv = inp["at_w_qkv"][0]
    perm = np.arange(1024).reshape(16, 64)
    perm = np.concatenate([perm[:, 32:], perm[:, :32]], axis=1).reshape(-1)
    for nm, base in (("wq", 0), ("wk", 1024)):
        w = wqkv[:, base:base + 1024]
        wp = w[:, perm]
        cols = []
        for c in range(8):
            cols += [w[:, c * 128:(c + 1) * 128], wp[:, c * 128:(c + 1) * 128]]
        parts[nm] = _col_items(np.concatenate(cols, axis=1), 8)
    parts["wv"] = _col_items(wqkv[:, 2048:3072], 4)
    parts["aout"] = _col_items(inp["at_w_out"][0], 4)
    wpack = np.concatenate([np.ascontiguousarray(parts[n]).reshape(-1) for n, _, _ in GROUPS]).astype(np.float32)
    assert wpack.size == WTOT

    cv = np.zeros((128, NCV), np.float32)
    def put8(name, vec):
        cv[:, CV[name]:CV[name] + 8] = vec.reshape(8, 128).T
    for l in range(2):
        put8(f"g_f1_{l}", inp["norm_ffn1"][l]); put8(f"g_mix_{l}", inp["norm_mix"][l]); put8(f"g_f2_{l}", inp["norm_ffn2"][l])
    put8("g_fin", inp["final_norm"])
    cv[:, CV["adw"]:CV["adw"] + 124] = inp["a_dw_w"][0].reshape(31, 4, 128).transpose(2, 0, 1).reshape(128, 124)
    cv[:, CV["bdw"]:CV["bdw"] + 12] = inp["b_dw_w"][0].reshape(3, 4, 128).transpose(2, 0, 1).reshape(128, 12)
    cv[:, CV["adb"]:CV["adb"] + 4] = inp["a_dw_b"][0].reshape(4, 128).T
    cv[:, CV["alg"]:CV["alg"] + 4] = inp["a_ln_g"][0].reshape(4, 128).T
    cv[:, CV["alb"]:CV["alb"] + 4] = inp["a_ln_b"][0].reshape(4, 128).T
    cv[:, CV["subg"]] = inp["subln_g"][0]
    for nm, key in (("lq1", "lam_q1"), ("lk1", "lam_k1"), ("lq2", "lam_q2"), ("lk2", "lam_k2")):
        cv[:, CV[nm]:CV[nm] + 64] = np.broadcast_to(inp[key][0][None, :], (128, 64))
    return wpack, cv


def _rope_tables(S):
    inv_freq = (1.0 / (10000.0 ** (np.arange(0, 64, 2, dtype=np.float32) / np.float32(64)))).astype(np.float32)
    pos = np.arange(S, dtype=np.float32)
    ang = (pos[:, None] * inv_freq[None, :]).astype(np.float32)
    cos = np.cos(ang).astype(np.float32); sin = np.sin(ang).astype(np.float32)
    p = np.arange(128)
    f = p % 32
    sign = np.where((p % 64) < 32, -1.0, 1.0).astype(np.float32)
    cosT = cos[:, f].T
    sinT = sin[:, f].T * sign[:, None]
    tab = np.stack([cosT.reshape(128, S // T, T), sinT.reshape(128, S // T, T)], axis=2)
    return np.ascontiguousarray(tab.astype(np.float32))


class _Op:
    __slots__ = ("eng", "emit", "deps", "sig", "count", "semkey", "dma_sem", "ndma")


class Sched:
    ENGS = ("pe", "act", "dve", "pool", "sp")

    def __init__(self):
        self.ops = {e: [] for e in self.ENGS}
        self.lastw = {}
        self.readers = {}

    def add(self, eng, emit, reads=(), writes=(), dma_sem=None, ndma=1):
        op = _Op()
        op.eng = eng; op.emit = emit; op.sig = False; op.count = 0
        op.dma_sem = dma_sem; op.ndma = ndma; op.semkey = dma_sem if dma_sem is not None else eng
        deps = set()
        lw = self.lastw; rd = self.readers
        for r in reads:
            w = lw.get(r)
            if w is not None:
                deps.add(w)
        for r in writes:
            w = lw.get(r)
            if w is not None:
                deps.add(w)
            for q in rd.get(r, ()):
                deps.add(q)
        deps.discard(op)
        op.deps = [d for d in deps if not (d.eng == "pe" and eng == "pe")]
        for d in op.deps:
            d.sig = True
        for r in reads:
            rd.setdefault(r, []).append(op)
        for r in writes:
            lw[r] = op
            rd[r] = []
        self.ops[eng].append(op)
        return op

    def assign(self):
        dcnt = {}
        for eng in self.ENGS:
            cnt = 0
            for op in self.ops[eng]:
                if op.dma_sem is not None:
                    c = dcnt.get(op.dma_sem, 0) + 16 * op.ndma
                    dcnt[op.dma_sem] = c
                    op.count = c
                elif op.sig:
                    cnt += 1
                    op.count = cnt

    def run(self, eng, e, sems):
        known = {}
        for op in self.ops[eng]:
            need = {}
            for d in op.deps:
                k = d.semkey
                if need.get(k, 0) < d.count:
                    need[k] = d.count
            for k, v in need.items():
                if known.get(k, 0) < v:
                    e.wait_ge(sems[k], v)
                    known[k] = v
            last = op.emit(e)
            if op.dma_sem is None and op.sig:
                last.then_inc(sems[eng], 1)


def build(NSEQ, NT, stop=None):
    S = NT * T
    NTT = NSEQ * NT
    nc = bass.Bass("TRN2", target_bir_lowering=False)
    x_d = nc.dram_tensor("x", [NSEQ * S, D], F32, kind="ExternalInput").ap()
    wpack_d = nc.dram_tensor("wpack", [WTOT], F32, kind="ExternalInput").ap()
    cvec_d = nc.dram_tensor("cvec", [128, NCV], F32, kind="ExternalInput").ap()
    ident_d = nc.dram_tensor("ident", [128, 128], F32, kind="ExternalInput").ap()
    rope_d = nc.dram_tensor("rope", [128, NT, 2, T], F32, kind="ExternalInput").ap()
    y_d = nc.dram_tensor("y", [NSEQ * S, D], F32, kind="ExternalOutput").ap()
    wsc = nc.dram_tensor("wsc", [WTOT], BF16, kind="Internal").ap()
    dsc = nc.dram_tensor("dsc", [9, 128, 2048], BF16, kind="Internal").ap()
    skind = "Internal" if stop is None else "ExternalOutput"
    xs_d = nc.dram_tensor("xs", [NTT, 128, 8 * T], F32, kind=skind).ap()
    qs_d = nc.dram_tensor("qs", [NTT, 128, 8 * T], BF16, kind=skind).ap()
    kt_d = nc.dram_tensor("kts", [NSEQ, 8, 128, S], BF16, kind=skind).ap()
    vs_d = nc.dram_tensor("vss", [NSEQ, 8, 128, S], BF16, kind=skind).ap()

    es = ExitStack()
    sb = lambda name, shape, dt: es.enter_context(nc.sbuf_tensor(name, shape, dt))
    xr = sb("xr", [128, 2, 8, T], F32)
    hT = sb("hT", [128, 8, T], BF16)
    mT = sb("mT", [128, 8, T], BF16)
    U = sb("U", [128, 24, T], BF16)
    sq = sb("sq", [128, 4, T], BF16)
    wr = sb("wr", [128, R_SLOTS, SLOT_ELEMS], BF16)
    axr = sb("axr", [128, 3, 4, 544], BF16)
    cxr = sb("cxr", [128, 3, 4, 516], BF16)
    gbr = sb("gbr", [128, 3, 4, T], BF16)
    PT = sb("PT", [128, 3, 2, T], BF16)
    TP = sb("TP", [128, 8, T], F32)
    tok = sb("tok", [128, 4, D], F32)
    ropeb = sb("ropeb", [128, 2, 2, T], F32)
    cv = sb("cv", [128, NCV], F32)
    identf = sb("identf", [128, 128], F32)
    identb = sb("identb", [128, 128], BF16)
    ones = sb("ones", [128, 4, 128], BF16)
    accD = sb("accD", [128, 2, T], F32)
    rstdb = sb("rstdb", [128, T], F32)
    onesf = sb("onesf", [128, 128], F32)
    sc = sb("sc", [128, 16], F32)
    ps = es.enter_context(nc.psum_tensor("ps", [128, 8, T], F32))

    sem_names = ["pe", "act", "dve", "pool"] + [f"w{i}" for i in range(R_SLOTS)] + \
        ["xl0", "xl1", "xst0", "xst1", "tl0", "tl1", "tl2", "tl3", "ts0", "ts1", "ts2", "ts3",
         "rope0", "rope1", "qld", "qst", "kst", "vst", "init", "dg0", "dg1", "dg2", "dg3", "dg4", "dg5", "cast0", "cast1", "cast2", "cast3"]
    sems = {n: es.enter_context(nc.semaphore(n)) for n in sem_names}

    sch = Sched()
    add = sch.add

    def bank(b):
        return ps[:, b, :]

    def col(name, k=0):
        return cv[:, CV[name] + k:CV[name] + k + 1]

    def scol(k):
        return sc[:, k:k + 1]

    tctr = [0]

    def tbuf():
        k = tctr[0] % 8
        tctr[0] += 1
        return k

    def act(out, in_, func, reads, writes, bias=None, scale=None):
        kw = {}
        if bias is not None:
            kw["bias"] = bias
        if scale is not None:
            kw["scale"] = scale
        add("act", lambda e, out=out, in_=in_, func=func, kw=kw: e.activation(out=out, in_=in_, func=func, **kw),
            reads=reads, writes=writes)

    def dve_tt(out, in0, in1, op, reads, writes):
        add("dve", lambda e, out=out, in0=in0, in1=in1, op=op: e.tensor_tensor(out=out, in0=in0, in1=in1, op=op),
            reads=reads, writes=writes)

    def dve_stt(out, in0, scalar, in1, op0, op1, reads, writes):
        add("dve", lambda e, out=out, in0=in0, scalar=scalar, in1=in1, op0=op0, op1=op1:
            e.scalar_tensor_tensor(out=out, in0=in0, scalar=scalar, in1=in1, op0=op0, op1=op1),
            reads=reads, writes=writes)

    def dve_ts(out, in0, s1, s2, op0, op1, reads, writes):
        if s2 is None:
            add("dve", lambda e, out=out, in0=in0, s1=s1, op0=op0:
                e.tensor_scalar(out=out, in0=in0, scalar1=s1, scalar2=None, op0=op0), reads=reads, writes=writes)
        else:
            add("dve", lambda e, out=out, in0=in0, s1=s1, s2=s2, op0=op0, op1=op1:
                e.tensor_scalar(out=out, in0=in0, scalar1=s1, scalar2=s2, op0=op0, op1=op1), reads=reads, writes=writes)

    def dve_recip(out, in_, reads, writes):
        add("dve", lambda e, out=out, in_=in_: e.reciprocal(out=out, in_=in_), reads=reads, writes=writes)

    def dve_copy(out, in_, reads, writes):
        add("dve", lambda e, out=out, in_=in_: e.tensor_copy(out=out, in_=in_), reads=reads, writes=writes)

    def dve_memset(ap, val, writes):
        add("dve", lambda e, ap=ap, val=val: e.memset(ap, val), writes=writes)

    def pe_mm(out, pairs, reads, writes, start=True, stop=True):
        def emit(e, out=out, pairs=pairs, start=start, stop=stop):
            n = len(pairs)
            last = None
            for k, (l, r) in enumerate(pairs):
                last = e.matmul(out, l, r, start=(start and k == 0), stop=(stop and k == n - 1))
            return last
        add("pe", emit, reads=reads, writes=writes)

    def dma(q, out, in_, sem, reads, writes):
        add(q, lambda e, out=out, in_=in_, sem=sem: e.dma_start(out=out, in_=in_).then_inc(sems[sem], 16),
            reads=reads, writes=writes, dma_sem=sem)

    wctr = [0]
    cast_done = set()

    def cast_group(g):
        off, n_items, ne = GOFF[g]
        for s0 in range(0, n_items, CAST_ITEMS):
            k = min(CAST_ITEMS, n_items - s0)
            a = off + s0 * ne * 128
            n = k * ne * 128
            cs = f"cast{len(cast_done) % 4}"
            cast_done.add((g, s0))
            dma("pool", wsc[a:a + n].rearrange("(p f) -> p f", p=128),
                wpack_d[a:a + n].rearrange("(p f) -> p f", p=128), cs,
                reads=[], writes=[("wsc", g, s0 // CAST_ITEMS), ("castsem", cs)])

    def load_item(srcs, reads):
        s = wctr[0] % R_SLOTS
        wctr[0] += 1
        def emit(e, srcs=srcs, s=s):
            last = None
            for ap, o in srcs:
                n = ap.shape[-1]
                last = e.dma_start(out=wr[:, s, o:o + n], in_=ap).then_inc(sems[f"w{s}"], 16)
            return last
        add("sp", emit, reads=reads, writes=[("w", s)], dma_sem=f"w{s}", ndma=len(srcs))
        return s

    def witem(g, idx):
        off, n_items, ne = GOFF[g]
        a = off + idx * ne * 128
        ap = wsc[a:a + ne * 128].rearrange("(p f) -> p f", p=128)
        return load_item([(ap, 0)], reads=[("wsc", g, idx // CAST_ITEMS)])

    def ditem(idx):
        n = 2048 if idx < 8 and idx % 2 == 0 else (1920 if idx < 8 else 1536)
        return load_item([(dsc[idx][:, 0:n], 0)], reads=[("dsc", idx)])

    def init_loads():
        def emit(e):
            e.dma_start(out=cv[:], in_=cvec_d[:, :]).then_inc(sems["init"], 16)
            return e.dma_start(out=identf[:], in_=ident_d[:, :]).then_inc(sems["init"], 16)
        add("pool", emit, writes=["cv", "identf"], dma_sem="init", ndma=2)

    def init():
        dve_copy(identb[:], identf[:], reads=["identf"], writes=["identb"])
        for k, v in enumerate([1.0 / 1024, 1.0 / 512, 1.0 / 128, 1.0]):
            dve_memset(ones[:, k, :], v, writes=["ones"])
        dve_memset(onesf[:], 1.0, writes=["ones"])
        dve_memset(sc[:, 8:9], NORM_EPS, writes=["sc"])
        dve_memset(sc[:, 9:10], LN_EPS, writes=["sc"])
        dve_memset(sc[:, 10:11], SUBLN_EPS, writes=["sc"])
        for j, (a, b) in enumerate((("lq1", "lk1"), ("lq2", "lk2"))):
            dve_tt(TP[:, 0, j * 64:(j + 1) * 64], cv[:, CV[a]:CV[a] + 64], cv[:, CV[b]:CV[b] + 64], ALU.mult,
                   reads=["cv"], writes=[("T", 0)])
            add("dve", lambda e, j=j: e.reduce_sum(out=sc[:, j:j + 1], in_=TP[:, 0, j * 64:(j + 1) * 64], axis=AX.X),
                reads=[("T", 0)], writes=["sc"])
        act(sc[:, 2:4], sc[:, 0:2], AF.Exp, reads=["sc"], writes=["sc"])
        dve_tt(sc[:, 4:5], sc[:, 3:4], sc[:, 2:3], ALU.subtract, reads=["sc"], writes=["sc"])
        dve_ts(sc[:, 11:12], sc[:, 4:5], -LAMBDA_INIT, None, ALU.add, None, reads=["sc"], writes=["sc"])
        dve_ts(sc[:, 12:13], col("subg"), 1.0 - LAMBDA_INIT, None, ALU.mult, None, reads=["cv", "sc"], writes=["sc"])
        Uf = U[:].rearrange("p c t -> p (c t)")
        for it in range(9):
            base = (it % 6) * 2048
            chunks = [("U", (it % 6) * 4 + q) for q in range(4)]
            if it < 8:
                ch, half = it // 2, it % 2
                taps = list(range(16)) if half == 0 else list(range(16, 31))
                cols = [CV["adw"] + k * 4 + ch for k in taps]
            else:
                cols = [CV["bdw"] + k * 4 + ch for ch in range(4) for k in range(3)]
            for m, cidx in enumerate(cols):
                dve_ts(Uf[:, base + m * 128:base + (m + 1) * 128], identb[:], cv[:, cidx:cidx + 1], None, ALU.mult, None,
                       reads=["identb", "cv"], writes=chunks)
            n = len(cols) * 128
            dma("sp", dsc[it][:, 0:n], Uf[:, base:base + n], f"dg{it % 6}", reads=chunks, writes=[("dsc", it)])

    pend = []

    def flush_pend():
        while pend:
            pend.pop(0)()

    def stat_chunk(X, c, delayed):
        k = c % 4
        act(sq[:, k, :], xr[:, X, c, :], AF.Square, reads=[("x", X, c)], writes=[("sq", k)])
        def mm(c=c, k=k):
            pe_mm(bank(6), [(ones[:, 0, :], sq[:, k, :])], reads=[("sq", k)], writes=[("ps", 6)],
                  start=(c == 0), stop=(c == 7))
        if delayed:
            pend.append(mm)
        else:
            mm()

    def stats_from_x(X):
        for c in range(8):
            stat_chunk(X, c, False)

    def norm_apply(X, gname, final=False):
        tr = tbuf()
        act(TP[:, tr, :], bank(6), AF.Ln, reads=[("ps", 6)], writes=[("T", tr)], bias=scol(8))
        act(TP[:, tr, :], TP[:, tr, :], AF.Exp, reads=[("T", tr)], writes=[("T", tr)], scale=-0.5)
        for c in range(8):
            if final:
                dve_stt(xr[:, X, c, :], xr[:, X, c, :], col(gname, c), TP[:, tr, :], ALU.mult, ALU.mult,
                        reads=[("x", X, c), ("T", tr)], writes=[("x", X, c)])
            else:
                dve_stt(hT[:, c, :], xr[:, X, c, :], col(gname, c), TP[:, tr, :], ALU.mult, ALU.mult,
                        reads=[("x", X, c), ("T", tr)], writes=[("h", c)])

    def prescale(X, i, next_g):
        add("act", lambda e, X=X, i=i, next_g=next_g: e.activation(
            out=hT[:, i, :], in_=xr[:, X, i, :], func=AF.Copy, scale=col(next_g, i)),
            reads=[("x", X, i)], writes=[("h", i)])

    def compute_rstd():
        act(rstdb[:], bank(6), AF.Ln, reads=[("ps", 6)], writes=["rstd"], bias=scol(8))
        act(rstdb[:], rstdb[:], AF.Exp, reads=["rstd"], writes=["rstd"], scale=-0.5)

    def resid_update(X, i, b, scale, next_g=None):
        dve_stt(xr[:, X, i, :], bank(b), scale, xr[:, X, i, :], ALU.mult, ALU.add,
                reads=[("ps", b), ("x", X, i)], writes=[("x", X, i)])
        if next_g is not None:
            prescale(X, i, next_g)
        stat_chunk(X, i, True)

    def ffn(X, ggu, gd, next_g=None):
        act(rstdb[:], bank(6), AF.Ln, reads=[("ps", 6)], writes=["rstd"], bias=scol(8))
        act(rstdb[:], rstdb[:], AF.Exp, reads=["rstd"], writes=["rstd"], scale=-0.5)
        for j in range(NFC):
            s = witem(ggu, j)
            bg, bu = j % 2, 2 + j % 2
            pe_mm(bank(bg), [(wr[:, s, c * 256:c * 256 + 128], hT[:, c, :]) for c in range(8)],
                  reads=[("w", s)] + [("h", c) for c in range(8)], writes=[("ps", bg)])
            pe_mm(bank(bu), [(wr[:, s, c * 256 + 128:c * 256 + 256], hT[:, c, :]) for c in range(8)],
                  reads=[("w", s)] + [("h", c) for c in range(8)], writes=[("ps", bu)])
            tg = tbuf(); tu = tbuf()
            dve_tt(TP[:, tg, :], bank(bg), rstdb[:], ALU.mult, reads=[("ps", bg), "rstd"], writes=[("T", tg)])
            act(TP[:, tg, :], TP[:, tg, :], AF.Silu, reads=[("T", tg)], writes=[("T", tg)])
            dve_tt(TP[:, tu, :], bank(bu), rstdb[:], ALU.mult, reads=[("ps", bu), "rstd"], writes=[("T", tu)])
            dve_tt(U[:, j, :], TP[:, tu, :], TP[:, tg, :], ALU.mult, reads=[("T", tu), ("T", tg)], writes=[("U", j)])
        for i in range(8):
            s = witem(gd, i)
            b = 4 + i % 2
            pe_mm(bank(b), [(wr[:, s, j * 128:(j + 1) * 128], U[:, j, :]) for j in range(NFC)],
                  reads=[("w", s)] + [("U", j) for j in range(NFC)], writes=[("ps", b)])
            if len(pend) > 0:
                pend.pop(0)()
            resid_update(X, i, b, 0.5, next_g)
        flush_pend()

    def proj_resid(X, g, src, srckey, next_g=None):
        for n in range(4):
            s = witem(g, n)
            for hh in range(2):
                i = 2 * n + hh
                b = 4 + i % 2
                pe_mm(bank(b), [(wr[:, s, kc * 256 + hh * 128:kc * 256 + hh * 128 + 128], src[:, kc, :]) for kc in range(8)],
                      reads=[("w", s)] + [(srckey, kc) for kc in range(8)], writes=[("ps", b)])
                if len(pend) > 0:
                    pend.pop(0)()
                resid_update(X, i, b, 1.0, next_g)
        flush_pend()

    def load_tok(seq, i):
        for blk in range(4):
            r0 = seq * S + i * T + blk * 128
            dma("pool", tok[:, blk, :], x_d[r0:r0 + 128, :], f"tl{blk}", reads=[], writes=[("tok", blk)])

    def load_x(seq, i, X):
        dma("pool", xr[:, X].rearrange("p c t -> p (c t)"), xs_d[seq * NT + i], f"xl{X}",
            reads=[("xs", seq, i)], writes=[("x", X, c) for c in range(8)])

    def store_x(seq, i, X):
        dma("pool", xs_d[seq * NT + i], xr[:, X].rearrange("p c t -> p (c t)"), f"xst{X}",
            reads=[("x", X, c) for c in range(8)], writes=[("xs", seq, i)])

    def load_rope(i, rb):
        dma("pool", ropeb[:, rb], rope_d[:, i], f"rope{rb}", reads=[], writes=[("rope", rb)])

    def load_q(seq, i):
        dma("pool", hT[:].rearrange("p c t -> p (c t)"), qs_d[seq * NT + i], "qld",
            reads=[("qs", seq, i)], writes=[("h", c) for c in range(8)])

    def stage_A(seq, i, X, prefetch):
        r = i % 3
        for blk in range(4):
            for half in range(2):
                b = 4 + half
                def emit(e, blk=blk, half=half, b=b):
                    last = None
                    for cc in range(4):
                        c = half * 4 + cc
                        last = e.transpose(ps[:, b, cc * 128:(cc + 1) * 128], tok[:, blk, c * 128:(c + 1) * 128], identf[:])
                    return last
                add("pe", emit, reads=[("tok", blk)], writes=[("ps", b)])
                dst = xr[:, X, half * 4:half * 4 + 4, blk * 128:(blk + 1) * 128]
                src = ps[:, b, :].rearrange("p (c t) -> p c t", c=4)
                wl = [("x", X, half * 4 + cc) for cc in range(4)]
                if half == 0:
                    add("act", lambda e, dst=dst, src=src: e.activation(out=dst, in_=src, func=AF.Copy),
                        reads=[("ps", b)], writes=wl)
                else:
                    dve_copy(dst, src, reads=[("ps", b)], writes=wl)
        prefetch("after_tok")
        for c in range(8):
            if c % 2 == 0:
                prescale(X, c, "g_f1_0")
            else:
                dve_ts(hT[:, c, :], xr[:, X, c, :], col("g_f1_0", c), None, ALU.mult, None,
                       reads=[("x", X, c)], writes=[("h", c)])
        stats_from_x(X)
        ffn(X, "f1gu0", "f1d0", next_g="g_mix_0")
        compute_rstd()
        hreads = [("h", c) for c in range(8)]
        for k in range(4):
            s = witem("cin", k)
            bv, bgt = k % 2, 2 + k % 2
            pe_mm(bank(bv), [(wr[:, s, c * 256:c * 256 + 128], hT[:, c, :]) for c in range(8)],
                  reads=[("w", s)] + hreads, writes=[("ps", bv)])
            pe_mm(bank(bgt), [(wr[:, s, c * 256 + 128:c * 256 + 256], hT[:, c, :]) for c in range(8)],
                  reads=[("w", s)] + hreads, writes=[("ps", bgt)])
            tg = tbuf(); tv = tbuf()
            dve_tt(TP[:, tg, :], bank(bgt), rstdb[:], ALU.mult, reads=[("ps", bgt), "rstd"], writes=[("T", tg)])
            act(TP[:, tg, :], TP[:, tg, :], AF.Sigmoid, reads=[("T", tg)], writes=[("T", tg)])
            dve_tt(TP[:, tv, :], bank(bv), rstdb[:], ALU.mult, reads=[("ps", bv), "rstd"], writes=[("T", tv)])
            dve_tt(axr[:, r, k, 15:15 + T], TP[:, tv, :], TP[:, tg, :], ALU.mult,
                   reads=[("T", tv), ("T", tg)], writes=[("ax", r)])
        for k in range(4):
            s = witem("cin", 4 + k)
            bc, bh = k % 2, 2 + k % 2
            pe_mm(bank(bc), [(wr[:, s, c * 256:c * 256 + 128], hT[:, c, :]) for c in range(8)],
                  reads=[("w", s)] + hreads, writes=[("ps", bc)])
            pe_mm(bank(bh), [(wr[:, s, c * 256 + 128:c * 256 + 256], hT[:, c, :]) for c in range(8)],
                  reads=[("w", s)] + hreads, writes=[("ps", bh)])
            t = tbuf()
            dve_tt(TP[:, t, :], bank(bc), rstdb[:], ALU.mult, reads=[("ps", bc), "rstd"], writes=[("T", t)])
            dve_tt(TP[:, t, :], TP[:, t, :], rstdb[:], ALU.mult, reads=[("T", t), "rstd"], writes=[("T", t)])
            dve_tt(cxr[:, r, k, 1:1 + T], TP[:, t, :], bank(bh), ALU.mult,
                   reads=[("T", t), ("ps", bh)], writes=[("cx", r)])
        for n in range(2):
            s = witem("cin", 8 + n)
            for hh in range(2):
                b = 2 * (n % 2) + hh
                pe_mm(bank(b), [(wr[:, s, c * 256 + hh * 128:c * 256 + hh * 128 + 128], hT[:, c, :]) for c in range(8)],
                      reads=[("w", s)] + hreads, writes=[("ps", b)])
                dve_tt(gbr[:, r, 2 * n + hh, :], bank(b), rstdb[:], ALU.mult, reads=[("ps", b), "rstd"], writes=[("gb", r)])
        if i == 0:
            dve_memset(axr[:, r, :, 0:15], 0.0, writes=[("ax", r)])
            dve_memset(cxr[:, r, :, 0:1], 0.0, writes=[("cx", r)])
        else:
            rp = (i - 1) % 3
            dve_copy(axr[:, rp, :, 15 + T:30 + T], axr[:, r, :, 15:30], reads=[("ax", r)], writes=[("ax", rp)])
            dve_copy(cxr[:, rp, :, 1 + T:2 + T], cxr[:, r, :, 1:2], reads=[("cx", r)], writes=[("cx", rp)])
        if i == NT - 1:
            dve_memset(axr[:, r, :, 15 + T:30 + T], 0.0, writes=[("ax", r)])
            dve_memset(cxr[:, r, :, 1 + T:2 + T], 0.0, writes=[("cx", r)])
        else:
            rn = (i + 1) % 3
            dve_copy(axr[:, rn, :, 0:15], axr[:, r, :, T:T + 15], reads=[("ax", r)], writes=[("ax", rn)])
            dve_copy(cxr[:, rn, :, 0:1], cxr[:, r, :, T:T + 1], reads=[("cx", r)], writes=[("cx", rn)])
        store_x(seq, i, X)

    def stage_B(seq, i, X, rb, prefetch):
        r = i % 3
        prefetch("start")
        for ch in range(4):
            s0 = ditem(2 * ch)
            s1 = ditem(2 * ch + 1)
            pairs = [(wr[:, s0, k * 128:(k + 1) * 128], axr[:, r, ch, k:k + T]) for k in range(16)]
            pairs += [(wr[:, s1, (k - 16) * 128:(k - 15) * 128], axr[:, r, ch, k:k + T]) for k in range(16, 31)]
            pe_mm(bank(ch), pairs, reads=[("w", s0), ("w", s1), ("ax", r)], writes=[("ps", ch)])
        for ch in range(4):
            k0, k1 = ch % 2, 2 + ch % 2
            act(sq[:, k0, :], bank(ch), AF.Identity, reads=[("ps", ch)], writes=[("sq", k0)], bias=col("adb", ch))
            pe_mm(bank(6), [(ones[:, 1, :], sq[:, k0, :])], reads=[("sq", k0)], writes=[("ps", 6)],
                  start=(ch == 0), stop=(ch == 3))
            act(sq[:, k1, :], bank(ch), AF.Square, reads=[("ps", ch)], writes=[("sq", k1)], bias=col("adb", ch))
            pe_mm(bank(7), [(ones[:, 1, :], sq[:, k1, :])], reads=[("sq", k1)], writes=[("ps", 7)],
                  start=(ch == 0), stop=(ch == 3))
        tm = tbuf(); tv = tbuf(); trs = tbuf()
        act(TP[:, tm, :], bank(6), AF.Copy, reads=[("ps", 6)], writes=[("T", tm)])
        dve_tt(TP[:, tv, :], TP[:, tm, :], TP[:, tm, :], ALU.mult, reads=[("T", tm)], writes=[("T", tv)])
        dve_tt(TP[:, tv, :], bank(7), TP[:, tv, :], ALU.subtract, reads=[("ps", 7), ("T", tv)], writes=[("T", tv)])
        dve_ts(TP[:, tv, :], TP[:, tv, :], 0.0, LN_EPS, ALU.max, ALU.add, reads=[("T", tv)], writes=[("T", tv)])
        act(TP[:, tv, :], TP[:, tv, :], AF.Ln, reads=[("T", tv)], writes=[("T", tv)])
        act(TP[:, trs, :], TP[:, tv, :], AF.Exp, reads=[("T", tv)], writes=[("T", trs)], scale=-0.5)
        for ch in range(4):
            td = tbuf()
            dve_stt(TP[:, td, :], bank(ch), col("adb", ch), TP[:, tm, :], ALU.add, ALU.subtract,
                    reads=[("ps", ch), ("T", tm)], writes=[("T", td)])
            dve_tt(TP[:, td, :], TP[:, td, :], TP[:, trs, :], ALU.mult, reads=[("T", td), ("T", trs)], writes=[("T", td)])
            act(mT[:, ch, :], TP[:, td, :], AF.Silu, reads=[("T", td)], writes=[("m", ch)],
                bias=col("alb", ch), scale=col("alg", ch))
        sB = ditem(8)
        for ch in range(4):
            b = 4 + ch % 2
            pe_mm(bank(b), [(wr[:, sB, (ch * 3 + k) * 128:(ch * 3 + k + 1) * 128], cxr[:, r, ch, k:k + T]) for k in range(3)],
                  reads=[("w", sB), ("cx", r)], writes=[("ps", b)])
            dve_tt(mT[:, 4 + ch, :], bank(b), gbr[:, r, ch, :], ALU.mult,
                   reads=[("ps", b), ("gb", r)], writes=[("m", 4 + ch)])
        proj_resid(X, "cout", mT, "m", next_g="g_f2_0")
        ffn(X, "f2gu0", "f2d0", next_g="g_f1_1")
        ffn(X, "f1gu1", "f1d1", next_g="g_mix_1")
        compute_rstd()
        for j_ in range(2):
            dve_tt(ropeb[:, rb, j_, :], ropeb[:, rb, j_, :], rstdb[:], ALU.mult,
                   reads=[("rope", rb), "rstd"], writes=[("rope", rb)])
        hreads = [("h", c) for c in range(8)]
        for which, g in enumerate(("wq", "wk")):
            for c in range(8):
                s = witem(g, c)
                b0, b1 = c % 2, 2 + c % 2
                pe_mm(bank(b0), [(wr[:, s, kc * 256:kc * 256 + 128], hT[:, kc, :]) for kc in range(8)],
                      reads=[("w", s)] + hreads, writes=[("ps", b0)])
                pe_mm(bank(b1), [(wr[:, s, kc * 256 + 128:kc * 256 + 256], hT[:, kc, :]) for kc in range(8)],
                      reads=[("w", s)] + hreads, writes=[("ps", b1)])
                t1 = tbuf(); t2 = tbuf()
                dve_tt(TP[:, t1, :], bank(b0), ropeb[:, rb, 0, :], ALU.mult, reads=[("ps", b0), ("rope", rb)], writes=[("T", t1)])
                dve_tt(TP[:, t2, :], bank(b1), ropeb[:, rb, 1, :], ALU.mult, reads=[("ps", b1), ("rope", rb)], writes=[("T", t2)])
                dve_tt(U[:, which * 8 + c, :], TP[:, t1, :], TP[:, t2, :], ALU.add,
                       reads=[("T", t1), ("T", t2)], writes=[("U", which * 8 + c)])
                if which == 0:
                    dve_tt(mT[:, c, :], hT[:, c, :], rstdb[:], ALU.mult, reads=[("h", c), "rstd"], writes=[("m", c)])
        for n in range(4):
            s = witem("wv", n)
            for bp in range(2):
                b = 4 + (2 * n + bp) % 2
                def emit(e, s=s, bp=bp, b=b):
                    last = None
                    for bb in range(2):
                        blk = 2 * bp + bb
                        for kc in range(8):
                            last = e.matmul(ps[:, b, bb * 256:(bb + 1) * 256], mT[:, kc, blk * 128:(blk + 1) * 128],
                                            wr[:, s, kc * 256:(kc + 1) * 256], start=(kc == 0), stop=(kc == 7))
                    return last
                add("pe", emit, reads=[("w", s)] + [("m", c_) for c_ in range(8)], writes=[("ps", b)])
                for bb in range(2):
                    blk = 2 * bp + bb
                    src = ps[:, b, bb * 256:(bb + 1) * 256].rearrange("p (h e) -> p h e", h=2)
                    dst = U[:, 16 + 2 * n:16 + 2 * n + 2, blk * 128:(blk + 1) * 128]
                    wl = [("U", 16 + 2 * n), ("U", 16 + 2 * n + 1)]
                    if bb == 0:
                        add("act", lambda e, dst=dst, src=src: e.activation(out=dst, in_=src, func=AF.Copy),
                            reads=[("ps", b)], writes=wl)
                    else:
                        dve_copy(dst, src, reads=[("ps", b)], writes=wl)
        dma("pool", qs_d[seq * NT + i], U[:, 0:8, :].rearrange("p c t -> p (c t)"), "qst",
            reads=[("U", c) for c in range(8)], writes=[("qs", seq, i)])
        dma("pool", kt_d[seq][:, :, i * T:(i + 1) * T].rearrange("h p t -> p h t"), U[:, 8:16, :], "kst",
            reads=[("U", c) for c in range(8, 16)], writes=[("K", seq, i)])
        dma("pool", vs_d[seq][:, :, i * T:(i + 1) * T].rearrange("h p t -> p h t"), U[:, 16:24, :], "vst",
            reads=[("U", c) for c in range(16, 24)], writes=[("V", seq, i)])
        store_x(seq, i, X)
        prefetch("end")

    def stage_C(seq, i, X, prefetch):
        prefetch("start")
        NKB = S // 128
        PCS = 8
        pctr = [0]
        deferred = []
        for h in range(8):
            qreads = [("U", h)]
            slots = {}
            def get_piece(pc, h=h, slots=slots):
                if pc not in slots:
                    k0 = pc * PCS * 128
                    n = min(PCS * 128, S - k0)
                    tiles = sorted(set([(k0 + j) // T for j in range(0, n, 128)]))
                    slots[pc] = load_item([(kt_d[seq][h][:, k0:k0 + n], 0), (vs_d[seq][h][:, k0:k0 + n], 1024)],
                                          reads=[("K", seq, t) for t in tiles] + [("V", seq, t) for t in tiles])
                return slots[pc]

            def emit_S(kb, h=h):
                s = get_piece(kb // PCS)
                kk = (kb % PCS) * 128
                sbuf_ = kb % 2
                def emit(e, s=s, kk=kk, sbuf_=sbuf_, h=h):
                    e.matmul(ps[:, 2 * sbuf_, :], wr[0:64, s, kk:kk + 128], hT[0:64, h, :], start=True, stop=True)
                    return e.matmul(ps[:, 2 * sbuf_ + 1, :], wr[64:128, s, kk:kk + 128], hT[64:128, h, :], start=True, stop=True)
                add("pe", emit, reads=[("w", s), ("h", h)], writes=[("ps", 2 * sbuf_), ("ps", 2 * sbuf_ + 1)])

            def emit_PV(kb, h=h):
                s = get_piece(kb // PCS)
                kk = 1024 + (kb % PCS) * 128
                sbuf_ = kb % 2
                pb = pctr[0] % 3
                pctr[0] += 1
                add("act", lambda e, pb=pb, sbuf_=sbuf_: e.activation(
                    out=PT[:, pb], in_=ps[:, 2 * sbuf_:2 * sbuf_ + 2, :], func=AF.Exp, scale=0.125),
                    reads=[("ps", 2 * sbuf_), ("ps", 2 * sbuf_ + 1)], writes=[("P", pb)])
                if kb + 2 < NKB:
                    emit_S(kb + 2)
                def emit(e, s=s, kk=kk, pb=pb, kb=kb):
                    st = (kb == 0); sp_ = (kb == NKB - 1)
                    e.matmul(ps[:, 4, :], wr[:, s, kk:kk + 128], PT[:, pb, 0, :], start=st, stop=sp_)
                    return e.matmul(ps[:, 5, :], wr[:, s, kk:kk + 128], PT[:, pb, 1, :], start=st, stop=sp_)
                add("pe", emit, reads=[("w", s), ("P", pb)], writes=[("ps", 4), ("ps", 5)])
                if kb % 4 == 3:
                    def emit2(e, pb=pb, kb=kb):
                        st = (kb == 3)
                        e.matmul(ps[:, 6, :], ones[:, 3, :], PT[:, pb, 0, :], start=st, stop=False)
                        return e.matmul(ps[:, 7, :], ones[:, 3, :], PT[:, pb, 1, :], start=st, stop=False)
                    add("pe", emit2, reads=[("P", pb)], writes=[("ps", 6), ("ps", 7)])
                elif kb == 0:
                    add("dve", lambda e, pb=pb: e.tensor_copy(out=accD[:], in_=PT[:, pb]),
                        reads=[("P", pb)], writes=["accD"])
                else:
                    add("dve", lambda e, pb=pb: e.tensor_tensor(out=accD[:], in0=accD[:], in1=PT[:, pb], op=ALU.add),
                        reads=[("P", pb), "accD"], writes=["accD"])

            emit_S(0)
            if NKB > 1:
                emit_S(1)
            for kb in range(NKB):
                emit_PV(kb)
                if kb == 1 and deferred:
                    deferred.pop(0)()
            o0 = tbuf(); o1 = tbuf()
            dve_copy(TP[:, o0, :], bank(4), reads=[("ps", 4)], writes=[("T", o0)])
            act(TP[:, o1, :], bank(5), AF.Copy, reads=[("ps", 5)], writes=[("T", o1)])
            k1 = 2 * (h % 2)
            add("act", lambda e, k1=k1: e.activation(out=sq[:, k1:k1 + 2, :], in_=accD[:], func=AF.Copy),
                reads=["accD"], writes=[("sq", k1), ("sq", k1 + 1)])
            for m_ in range(2):
                def emit3(e, m_=m_, k1=k1):
                    return e.matmul(ps[:, 6 + m_, :], ones[:, 3, :], sq[:, k1 + m_, :], start=(NKB < 4), stop=True)
                add("pe", emit3, reads=[("sq", k1 + m_)], writes=[("ps", 6 + m_)])

            def part2(h=h, o0=o0, o1=o1):
                r0 = tbuf(); r1 = tbuf(); to = tbuf(); tl = tbuf()
                act(TP[:, r0, :], bank(6), AF.Ln, reads=[("ps", 6)], writes=[("T", r0)])
                act(TP[:, r0, :], TP[:, r0, :], AF.Exp, reads=[("T", r0)], writes=[("T", r0)], scale=-1.0)
                act(TP[:, r1, :], bank(7), AF.Ln, reads=[("ps", 7)], writes=[("T", r1)])
                act(TP[:, r1, :], TP[:, r1, :], AF.Exp, reads=[("T", r1)], writes=[("T", r1)], scale=-1.0)
                dve_tt(TP[:, o0, :], TP[:, o0, :], TP[:, r0, :], ALU.mult, reads=[("T", o0), ("T", r0)], writes=[("T", o0)])
                dve_tt(TP[:, o1, :], TP[:, o1, :], TP[:, r1, :], ALU.mult, reads=[("T", o1), ("T", r1)], writes=[("T", o1)])
                dve_stt(TP[:, to, :], TP[:, o1, :], scol(11), TP[:, o0, :], ALU.mult, ALU.add,
                        reads=[("T", o1), ("T", o0)], writes=[("T", to)])
                k = (2 * (h % 2) + 2) % 4
                dve_tt(sq[:, k, :], TP[:, to, :], TP[:, to, :], ALU.mult, reads=[("T", to)], writes=[("sq", k)])
                pe_mm(bank(6), [(ones[:, 2, :], sq[:, k, :])], reads=[("sq", k)], writes=[("ps", 6)])
                act(TP[:, tl, :], bank(6), AF.Ln, reads=[("ps", 6)], writes=[("T", tl)], bias=scol(10))
                act(TP[:, tl, :], TP[:, tl, :], AF.Exp, reads=[("T", tl)], writes=[("T", tl)], scale=-0.5)
                dve_stt(mT[:, h, :], TP[:, to, :], scol(12), TP[:, tl, :], ALU.mult, ALU.mult,
                        reads=[("T", to), ("T", tl)], writes=[("m", h)])
            deferred.append(part2)
        while deferred:
            deferred.pop(0)()
        proj_resid(X, "aout", mT, "m", next_g="g_f2_1")
        if stop == "C":
            store_x(seq, i, X)
            dma("pool", qs_d[seq * NT + i], mT[:].rearrange("p c t -> p (c t)"), "qst",
                reads=[("m", c) for c in range(8)], writes=[("qs", seq, i)])
            prefetch("after_ffn")
            prefetch("end")
            return
        ffn(X, "f2gu1", "f2d1")
        prefetch("after_ffn")
        norm_apply(X, "g_fin", final=True)
        for blk in range(4):
            for half in range(2):
                b = 4 + half
                def emit(e, blk=blk, half=half, b=b):
                    last = None
                    for cc in range(4):
                        c = half * 4 + cc
                        last = e.transpose(ps[:, b, cc * 128:(cc + 1) * 128], xr[:, X, c, blk * 128:(blk + 1) * 128], identf[:])
                    return last
                add("pe", emit, reads=[("x", X, half * 4 + cc) for cc in range(4)], writes=[("ps", b)])
                dst = tok[:, blk, half * 512:(half + 1) * 512]
                if half == 0:
                    add("act", lambda e, dst=dst, b=b: e.activation(out=dst, in_=ps[:, b, :], func=AF.Copy),
                        reads=[("ps", b)], writes=[("tokh", blk, 0)])
                else:
                    dve_copy(dst, ps[:, b, :], reads=[("ps", b)], writes=[("tokh", blk, 1)])
            r0 = seq * S + i * T + blk * 128
            dma("pool", y_d[r0:r0 + 128, :], tok[:, blk, :], f"ts{blk}",
                reads=[("tok", blk), ("tokh", blk, 0), ("tokh", blk, 1)], writes=[("y", seq, i, blk), ("tok", blk)])
        prefetch("end")

    calls = []
    for seq in range(NSEQ):
        order = []
        for i in range(NT):
            order.append(("A", i))
            if i >= 1:
                order.append(("B", i - 1))
        order.append(("B", NT - 1))
        for i in range(NT):
            order.append(("C", i))
        calls += [(k, seq, i) for k, i in order]
    if stop == "A":
        calls = [c for c in calls if c[0] == "A"]
    elif stop == "B":
        calls = [c for c in calls if c[0] in ("A", "B")]

    init_loads()
    load_tok(calls[0][1], calls[0][2])
    for g, _, _ in GROUPS:
        if GROUP_STAGE[g] == "A":
            cast_group(g)
    init()
    add("pe", lambda e: e.nop(), reads=["identf", "identb", "ones", "cv", "sc"])
    add("act", lambda e: e.nop(), reads=["identf", "identb", "ones", "cv", "sc"])
    nB = [0]
    bcast = [False]
    ccast = [False]

    for n, (kind, seq, i) in enumerate(calls):
        X = n % 2
        nxt = calls[n + 1] if n + 1 < len(calls) else None
        rb_cur = nB[0] % 2

        def prefetch(point, kind=kind, nxt=nxt, n=n):
            if nxt is None:
                return
            nk, ns, ni = nxt
            NX = (n + 1) % 2
            if nk == "A":
                want = {"A": "after_tok", "B": "start", "C": "end"}[kind]
                if point == want:
                    load_tok(ns, ni)
            elif nk == "B":
                if point == "start" or (kind == "A" and point == "after_tok"):
                    load_x(ns, ni, NX)
                    load_rope(ni, (nB[0] + (1 if kind == "B" else 0)) % 2)
            elif nk == "C":
                if point == "start":
                    load_x(ns, ni, NX)
                if (kind == "B" and point == "end") or (kind == "C" and point == "after_ffn"):
                    load_q(ns, ni)
            if kind == "A" and point == "after_tok" and not bcast[0]:
                bcast[0] = True
                for g, _, _ in GROUPS:
                    if GROUP_STAGE[g] == "B":
                        cast_group(g)
            elif kind == "A" and point == "after_tok" and bcast[0] and not ccast[0] and n >= 2:
                ccast[0] = True
                for g, _, _ in GROUPS:
                    if GROUP_STAGE[g] == "C":
                        cast_group(g)

        if kind == "A":
            stage_A(seq, i, X, prefetch)
        elif kind == "B":
            stage_B(seq, i, X, rb_cur, prefetch)
            nB[0] += 1
        else:
            if not ccast[0]:
                ccast[0] = True
                for g, _, _ in GROUPS:
                    if GROUP_STAGE[g] == "C":
                        cast_group(g)
            stage_C(seq, i, X, prefetch)

    fin = [("tok", b) for b in range(4)]
    if stop is not None:
        fin += [("xs", q, w) for q in range(NSEQ) for w in range(NT)] + [("qs", q, w) for q in range(NSEQ) for w in range(NT)]
        fin += [("K", q, w) for q in range(NSEQ) for w in range(NT)] + [("V", q, w) for q in range(NSEQ) for w in range(NT)]
    add("pool", lambda e: None, reads=fin, writes=fin)

    sch.assign()
    with nc.Block() as block:
        @block.tensor
        def _(e):
            sch.run("pe", e, sems)

        @block.scalar
        def _(e):
            sch.run("act", e, sems)

        @block.vector
        def _(e):
            sch.run("dve", e, sems)

        @block.gpsimd
        def _(e):
            sch.run("pool", e, sems)

        @block.sync
        def _(e):
            sch.run("sp", e, sems)
    es.close()
    return nc


_IDENT = np.eye(128, dtype=np.float32)


def run_cores(xs_per_core, inp, NSEQ, NT, stop=None):
    wpack, cvec = _host_pack(inp)
    rope = _rope_tables(NT * T)
    nc = build(NSEQ, NT, stop)
    in_maps = [{"x": np.ascontiguousarray(xc.reshape(NSEQ * NT * T, D)), "wpack": wpack, "cvec": cvec,
                "ident": _IDENT, "rope": rope} for xc in xs_per_core]
    res = run_bass_kernel_spmd(nc, in_maps, core_ids=list(range(len(xs_per_core))))
    if stop is not None:
        return res.results
    return [r["y"].reshape(NSEQ, NT * T, D) for r in res.results]


def kernel(**inputs):
    inp = {k: np.asarray(v) for k, v in inputs.items()}
    xp, xsm = inp["x_prompt"], inp["x_sample"]
    per_core = [np.stack([xp[c], xsm[2 * c], xsm[2 * c + 1]], axis=0) for c in range(8)]
    outs = run_cores(per_core, inp, 3, 8)
    y_prompt = np.stack([outs[c][0] for c in range(8)], axis=0).astype(np.float32)
    y_sample = np.stack([outs[c // 2][1 + c % 2] for c in range(16)], axis=0).astype(np.float32)
    return (y_prompt, y_sample)
```
